# Optimizing a Trainium2 kernel written in Bass

```python
import math
import jax
import jax.numpy as jnp
from jax import lax
import numpy as np

D_MODEL = 1024
BATCH = 2
SEQ = 8192
DEPTH = 2

GRID_W = 64
CTX_LEN = 256
Q_BLOCK = 128
WINDOW = 128
BAND = Q_BLOCK + 2 * WINDOW
N_BRANCH = 4
N_MOD = 6
EPS = 1e-6
ROPE_THETA = 10000.0
NEG_INF = -1e30
HEAD_DIM = 64
DIFF_HEADS = 4
DIFF_QK_DIM = 64
DIFF_V_DIM = 2 * DIFF_QK_DIM
MLA_HEADS = 8
MLA_Q_LORA = 256
MLA_KV_LORA = 256
MLA_NOPE_DIM = 64
MLA_ROPE_DIM = 32
MLA_V_DIM = 64
GQA_HEADS = 8
GQA_KV_HEADS = 2
SWA_HEADS = 8
SWA_KV_HEADS = 2
BRANCH_WIDTH = 512
D_FF = 2816
SPLIT_WIDTHS = (
    DIFF_HEADS * 2 * DIFF_QK_DIM,
    DIFF_HEADS * 2 * DIFF_QK_DIM,
    DIFF_HEADS * DIFF_V_DIM,
    MLA_Q_LORA,
    MLA_KV_LORA + MLA_ROPE_DIM,
    GQA_HEADS * HEAD_DIM, GQA_KV_HEADS * HEAD_DIM, GQA_KV_HEADS * HEAD_DIM,
    SWA_HEADS * HEAD_DIM, SWA_KV_HEADS * HEAD_DIM, SWA_KV_HEADS * HEAD_DIM,
    N_BRANCH * D_MODEL,
)
IN_WIDTH = sum(SPLIT_WIDTHS)
f32 = jnp.float32

kernel_name = 'hybrid_dit_prefix_block'


def rms_norm(x, gain=None):
    xf = x.astype(f32)
    y = xf * lax.rsqrt(jnp.mean(xf * xf, axis=-1, keepdims=True) + EPS)
    if gain is not None:
        y = y * gain.astype(f32)
    return y.astype(x.dtype)


def modulate(x, shift, scale):
    return rms_norm(x) * (1.0 + scale) + shift


def axial_rope_tables(row, col, rot_dim):
    axis_dim = rot_dim // 2
    inv = ROPE_THETA ** (-jnp.arange(0, axis_dim, 2, dtype=f32) / axis_dim)
    ang = jnp.concatenate([row[:, None].astype(f32) * inv, col[:, None].astype(f32) * inv], axis=-1)
    return jnp.cos(ang), jnp.sin(ang)


def rope_2d(x, cos, sin):
    xf = x.astype(f32).reshape(*x.shape[:-1], x.shape[-1] // 2, 2)
    x0, x1 = xf[..., 0], xf[..., 1]
    c, s = cos[:, None, :], sin[:, None, :]
    return jnp.stack([x0 * c - x1 * s, x0 * s + x1 * c], axis=-1).reshape(x.shape).astype(x.dtype)


def split_columns(z):
    out, start = [], 0
    for w in SPLIT_WIDTHS:
        out.append(z[..., start:start + w])
        start += w
    return out


def to_blocks(t):
    B, N = t.shape[:2]
    return jnp.moveaxis(t.reshape(B, N // Q_BLOCK, Q_BLOCK, *t.shape[2:]), 1, 0)


def from_blocks(t):
    nb, B = t.shape[:2]
    return jnp.moveaxis(t, 0, 1).reshape(B, nb * Q_BLOCK, *t.shape[3:])


def sweep_query_blocks(fn, q):
    return from_blocks(lax.map(fn, to_blocks(q)))


def window_band(x):
    B, N = x.shape[:2]
    xp = jnp.pad(x, [(0, 0), (WINDOW, WINDOW)] + [(0, 0)] * (x.ndim - 2))
    xb = xp.reshape(B, N // Q_BLOCK + 2, Q_BLOCK, *x.shape[2:])
    return jnp.concatenate([xb[:, :-2], xb[:, 1:-1], xb[:, 2:]], axis=2)


def gqa_attend(q, k, v, scale):
    s = jnp.einsum('bqhgd,bkhd->bhgqk', q, k).astype(f32) * scale
    p = jax.nn.softmax(s, axis=-1).astype(v.dtype)
    return jnp.einsum('bhgqk,bkhd->bqhgd', p, v)


def softmax_with_sink(s, sink):
    col = jnp.broadcast_to(sink.astype(f32)[None, :, :, None, None], s.shape[:-1] + (1,))
    return jax.nn.softmax(jnp.concatenate([col, s], axis=-1), axis=-1)[..., 1:]


def diff_attend(q, k, v, lam):
    s = jnp.einsum('bqhmd,bkhmd->bhmqk', q, k).astype(f32) * (DIFF_QK_DIM ** -0.5)
    p = jax.nn.softmax(s, axis=-1)
    pd = (p[:, :, 0] - lam * p[:, :, 1]).astype(v.dtype)
    return jnp.einsum('bhqk,bkhd->bqhd', pd, v)


def diff_attention(q, k, v, qc, kc, vc, lam_vec, subln, layer_idx, rope, with_ctx_out):
    H, d, dv = DIFF_HEADS, DIFF_QK_DIM, DIFF_V_DIM
    B, N, _ = q.shape
    L = kc.shape[1]
    lam_init = 0.8 - 0.6 * math.exp(-0.3 * layer_idx)
    lv = lam_vec.astype(f32)
    lam = jnp.exp(jnp.sum(lv[0] * lv[1])) - jnp.exp(jnp.sum(lv[2] * lv[3])) + lam_init
    qh = rope_2d(q.reshape(B, N, 2 * H, d), *rope).reshape(B, N, H, 2, d)
    kh = rope_2d(k.reshape(B, N, 2 * H, d), *rope).reshape(B, N, H, 2, d)
    kch = kc.reshape(B, L, H, 2, d)
    vch = vc.reshape(B, L, H, dv)
    k_all = jnp.concatenate([kch, kh], axis=1)
    v_all = jnp.concatenate([vch, v.reshape(B, N, H, dv)], axis=1)

    def finish(o):
        return (rms_norm(o, subln) * (1.0 - lam_init)).reshape(o.shape[0], o.shape[1], H * dv)

    out = finish(sweep_query_blocks(lambda qi: diff_attend(qi, k_all, v_all, lam), qh))
    out_c = None
    if with_ctx_out:
        out_c = finish(diff_attend(qc.reshape(B, L, H, 2, d), kch, vch, lam))
    return out, out_c


def mla_attention(dq, dkv, dqc, dkvc, q_norm, kv_norm, w_uq, w_ukv, rope, with_ctx_out):
    H, dn, dr, dv = MLA_HEADS, MLA_NOPE_DIM, MLA_ROPE_DIM, MLA_V_DIM
    scale = (dn + dr) ** -0.5

    def queries(t, rope):
        B, T, _ = t.shape
        q = (rms_norm(t, q_norm) @ w_uq).reshape(B, T, H, dn + dr)
        if rope is not None:
            q = jnp.concatenate([q[..., :dn], rope_2d(q[..., dn:], *rope)], axis=-1)
        return q[:, :, :, None, :]

    def keys_values(t, rope):
        B, T, _ = t.shape
        c_kv, k_rot = t[..., :MLA_KV_LORA], t[..., MLA_KV_LORA:]
        kv = (rms_norm(c_kv, kv_norm) @ w_ukv).reshape(B, T, H, dn + dv)
        k_rot = k_rot[:, :, None, :]
        if rope is not None:
            k_rot = rope_2d(k_rot, *rope)
        k = jnp.concatenate([kv[..., :dn], jnp.broadcast_to(k_rot, (B, T, H, dr))], axis=-1)
        return k, kv[..., dn:]

    B, N, _ = dq.shape
    L = dkvc.shape[1]
    k, v = keys_values(dkv, rope)
    kc, vc = keys_values(dkvc, None)
    k_all = jnp.concatenate([kc, k], axis=1)
    v_all = jnp.concatenate([vc, v], axis=1)
    out = sweep_query_blocks(lambda qi: gqa_attend(qi, k_all, v_all, scale), queries(dq, rope)).reshape(B, N, H * dv)
    out_c = None
    if with_ctx_out:
        out_c = gqa_attend(queries(dqc, None), kc, vc, scale).reshape(B, L, H * dv)
    return out, out_c


def grid_attention(q, k, v, qc, kc, vc, q_gain, k_gain, rope, with_ctx_out):
    H, Hkv, d = GQA_HEADS, GQA_KV_HEADS, HEAD_DIM
    G = H // Hkv
    scale = d ** -0.5
    B, N, _ = q.shape
    L = kc.shape[1]
    qh = rope_2d(rms_norm(q.reshape(B, N, H, d), q_gain), *rope).reshape(B, N, Hkv, G, d)
    kh = rope_2d(rms_norm(k.reshape(B, N, Hkv, d), k_gain), *rope)
    kch = rms_norm(kc.reshape(B, L, Hkv, d), k_gain)
    vch = vc.reshape(B, L, Hkv, d)
    k_all = jnp.concatenate([kch, kh], axis=1)
    v_all = jnp.concatenate([vch, v.reshape(B, N, Hkv, d)], axis=1)
    out = sweep_query_blocks(lambda qi: gqa_attend(qi, k_all, v_all, scale), qh).reshape(B, N, H * d)
    out_c = None
    if with_ctx_out:
        qch = rms_norm(qc.reshape(B, L, H, d), q_gain).reshape(B, L, Hkv, G, d)
        out_c = gqa_attend(qch, kch, vch, scale).reshape(B, L, H * d)
    return out, out_c


def window_attention(q, k, v, qc, kc, vc, sink, rope, with_ctx_out):
    H, Hkv, d = SWA_HEADS, SWA_KV_HEADS, HEAD_DIM
    G = H // Hkv
    scale = d ** -0.5
    B, N, _ = q.shape
    L = kc.shape[1]
    sink = sink.reshape(Hkv, G)
    qh = rope_2d(q.reshape(B, N, H, d), *rope).reshape(B, N, Hkv, G, d)
    kh = rope_2d(k.reshape(B, N, Hkv, d), *rope)
    kch = kc.reshape(B, L, Hkv, d)
    vch = vc.reshape(B, L, Hkv, d)
    kb = jnp.moveaxis(window_band(kh), 1, 0)
    vb = jnp.moveaxis(window_band(v.reshape(B, N, Hkv, d)), 1, 0)
    i_idx = jnp.arange(Q_BLOCK)[:, None]
    j_idx = jnp.arange(BAND)[None, :]

    def block(args):
        qi, ki, vi, n = args
        s_ctx = jnp.einsum('bqhgd,bkhd->bhgqk', qi, kch).astype(f32) * scale
        s_band = jnp.einsum('bqhgd,bkhd->bhgqk', qi, ki).astype(f32) * scale
        kpos = n * Q_BLOCK - WINDOW + j_idx
        valid = (jnp.abs(j_idx - WINDOW - i_idx) <= WINDOW) & (kpos >= 0) & (kpos < N)
        s_band = jnp.where(valid, s_band, NEG_INF)
        p = softmax_with_sink(jnp.concatenate([s_ctx, s_band], axis=-1), sink).astype(vi.dtype)
        return (jnp.einsum('bhgqk,bkhd->bqhgd', p[..., :L], vch)
                + jnp.einsum('bhgqk,bkhd->bqhgd', p[..., L:], vi))

    out = from_blocks(lax.map(block, (to_blocks(qh), kb, vb, jnp.arange(N // Q_BLOCK)))).reshape(B, N, H * d)
    out_c = None
    if with_ctx_out:
        s = jnp.einsum('bqhgd,bkhd->bhgqk', qc.reshape(B, L, Hkv, G, d), kch).astype(f32) * scale
        p = softmax_with_sink(s, sink).astype(vch.dtype)
        out_c = jnp.einsum('bhgqk,bkhd->bqhgd', p, vch).reshape(B, L, H * d)
    return out, out_c


def merge_branches(outs, g, w_branch, w_out):
    B, T, _ = g.shape
    y = jnp.einsum('btke,ked->btkd', jnp.stack(outs, axis=2), w_branch)
    gate = jax.nn.sigmoid(g.reshape(B, T, N_BRANCH, D_MODEL))
    return jnp.sum(gate * y, axis=2) @ w_out


def token_mixing(h, hc, lp, layer_idx, rope64, rope32, with_ctx_out):
    aq, ak, av, bq, bkv, cq, ck, cv, dq, dk, dv, g = split_columns(h @ lp['w_in'])
    aqc, akc, avc, bqc, bkvc, cqc, ckc, cvc, dqc, dkc, dvc, gc = split_columns(hc @ lp['w_in'])
    oa, oa_c = diff_attention(aq, ak, av, aqc, akc, avc, lp['diff_lambda'], lp['diff_subln'], layer_idx, rope64, with_ctx_out)
    ob, ob_c = mla_attention(bq, bkv, bqc, bkvc, lp['mla_q_norm'], lp['mla_kv_norm'], lp['mla_w_uq'], lp['mla_w_ukv'], rope32, with_ctx_out)
    oc, oc_c = grid_attention(cq, ck, cv, cqc, ckc, cvc, lp['gqa_q_norm'], lp['gqa_k_norm'], rope64, with_ctx_out)
    od, od_c = window_attention(dq, dk, dv, dqc, dkc, dvc, lp['swa_sink'], rope64, with_ctx_out)
    out = merge_branches([oa, ob, oc, od], g, lp['w_branch'], lp['w_out'])
    out_c = None
    if with_ctx_out:
        out_c = merge_branches([oa_c, ob_c, oc_c, od_c], gc, lp['w_branch'], lp['w_out'])
    return out, out_c


def conv_ffn(h, lp):
    u = h @ lp['ffn_w_up']
    T = u.shape[1]
    w = lp['ffn_conv_w']
    up = jnp.pad(u, ((0, 0), (1, 1), (0, 0)))
    u = up[:, :T] * w[0] + up[:, 1:T + 1] * w[1] + up[:, 2:] * w[2] + lp['ffn_conv_b']
    val, gate = jnp.split(u, 2, axis=-1)
    return (jax.nn.silu(gate) * val) @ lp['ffn_w_down']


def hybrid_layer(x, xc, c, c_ctx, lp, layer_idx, rope64, rope32, update_ctx):
    B = x.shape[0]
    mod = (jax.nn.silu(c) @ lp['w_mod'] + lp['b_mod']).reshape(B, 1, N_MOD, D_MODEL)
    mod_c = (jax.nn.silu(c_ctx) @ lp['w_mod'] + lp['b_mod']).reshape(1, 1, N_MOD, D_MODEL)
    h = modulate(x, mod[:, :, 0], mod[:, :, 1])
    hc = modulate(xc, mod_c[:, :, 0], mod_c[:, :, 1])
    mix, mix_c = token_mixing(h, hc, lp, layer_idx, rope64, rope32, update_ctx)
    x = x + mod[:, :, 2] * mix
    x = x + mod[:, :, 5] * conv_ffn(modulate(x, mod[:, :, 3], mod[:, :, 4]), lp)
    if update_ctx:
        xc = xc + mod_c[:, :, 2] * mix_c
        xc = xc + mod_c[:, :, 5] * conv_ffn(modulate(xc, mod_c[:, :, 3], mod_c[:, :, 4]), lp)
    return x, xc


def setup_inputs(seed: int = 0) -> dict:
    key = jax.random.key(seed)
    ks = jax.random.split(key, 23)

    def nrm(i, shape, scale):
        return scale * jax.random.normal(ks[i], shape, jnp.float32)

    def gain(i, shape):
        return 1.0 + nrm(i, shape, 0.02)

    return {
        'x': nrm(0, (BATCH, SEQ, D_MODEL), 1.0),
        'c': nrm(1, (BATCH, D_MODEL), 1.0),
        'ctx': nrm(2, (BATCH, CTX_LEN, D_MODEL), 1.0),
        'c_ctx': nrm(3, (D_MODEL,), 1.0),
        'w_mod': nrm(4, (DEPTH, D_MODEL, N_MOD * D_MODEL), 0.3 * D_MODEL ** -0.5),
        'b_mod': nrm(5, (DEPTH, N_MOD * D_MODEL), 0.02),
        'w_in': nrm(6, (DEPTH, D_MODEL, IN_WIDTH), D_MODEL ** -0.5),
        'diff_lambda': nrm(7, (DEPTH, 4, DIFF_QK_DIM), 0.1),
        'diff_subln': gain(8, (DEPTH, DIFF_V_DIM)),
        'mla_q_norm': gain(9, (DEPTH, MLA_Q_LORA)),
        'mla_kv_norm': gain(10, (DEPTH, MLA_KV_LORA)),
        'mla_w_uq': nrm(11, (DEPTH, MLA_Q_LORA, MLA_HEADS * (MLA_NOPE_DIM + MLA_ROPE_DIM)), MLA_Q_LORA ** -0.5),
        'mla_w_ukv': nrm(12, (DEPTH, MLA_KV_LORA, MLA_HEADS * (MLA_NOPE_DIM + MLA_V_DIM)), MLA_KV_LORA ** -0.5),
        'gqa_q_norm': gain(13, (DEPTH, HEAD_DIM)),
        'gqa_k_norm': gain(14, (DEPTH, HEAD_DIM)),
        'swa_sink': nrm(15, (DEPTH, SWA_HEADS), 0.5),
        'w_branch': nrm(16, (DEPTH, N_BRANCH, BRANCH_WIDTH, D_MODEL), BRANCH_WIDTH ** -0.5),
        'w_out': nrm(17, (DEPTH, D_MODEL, D_MODEL), D_MODEL ** -0.5),
        'ffn_w_up': nrm(18, (DEPTH, D_MODEL, 2 * D_FF), D_MODEL ** -0.5),
        'ffn_conv_w': nrm(19, (DEPTH, 3, 2 * D_FF), 0.5),
        'ffn_conv_b': nrm(20, (DEPTH, 2 * D_FF), 0.02),
        'ffn_w_down': nrm(21, (DEPTH, D_FF, D_MODEL), D_FF ** -0.5),
        'final_norm': gain(22, (D_MODEL,)),
    }


def reference(x, c, ctx, c_ctx, w_mod, b_mod, w_in, diff_lambda, diff_subln, mla_q_norm, mla_kv_norm,
              mla_w_uq, mla_w_ukv, gqa_q_norm, gqa_k_norm, swa_sink, w_branch, w_out, ffn_w_up,
              ffn_conv_w, ffn_conv_b, ffn_w_down, final_norm):
    N = x.shape[1]
    n_rows = N // GRID_W
    row = jnp.repeat(jnp.arange(n_rows, dtype=jnp.int32), GRID_W)
    col = jnp.tile(jnp.arange(GRID_W, dtype=jnp.int32), n_rows)
    rope64 = axial_rope_tables(row, col, HEAD_DIM)
    rope32 = axial_rope_tables(row, col, MLA_ROPE_DIM)
    xc = ctx
    for l in range(DEPTH):
        lp = dict(w_mod=w_mod[l], b_mod=b_mod[l], w_in=w_in[l], diff_lambda=diff_lambda[l],
                  diff_subln=diff_subln[l], mla_q_norm=mla_q_norm[l], mla_kv_norm=mla_kv_norm[l],
                  mla_w_uq=mla_w_uq[l], mla_w_ukv=mla_w_ukv[l], gqa_q_norm=gqa_q_norm[l],
                  gqa_k_norm=gqa_k_norm[l], swa_sink=swa_sink[l], w_branch=w_branch[l], w_out=w_out[l],
                  ffn_w_up=ffn_w_up[l], ffn_conv_w=ffn_conv_w[l], ffn_conv_b=ffn_conv_b[l],
                  ffn_w_down=ffn_w_down[l])
        x, xc = hybrid_layer(x, xc, c, c_ctx, lp, l, rope64, rope32, l < DEPTH - 1)
    return rms_norm(x, final_norm)
```

```python
import math
from contextlib import ExitStack
import numpy as np
import ml_dtypes
import concourse.bass as bass
import concourse.mybir as mybir
from concourse.bass_utils import run_bass_kernel_spmd

F32 = mybir.dt.float32
BF16 = mybir.dt.bfloat16
AF = mybir.ActivationFunctionType
ALU = mybir.AluOpType

D = 1024
NL = 2
NT = 2048
NC = 256
TOK = NT + NC
CH = [(0, 512), (512, 512), (1024, 512), (1536, 512), (2048, 256)]
EPS = 1e-6
NKB = 66
KTROWS = 1536
VW = 1280
RANKROWS = KTROWS + (NT * VW) // NT
DFF = 2816
NEGM = -30000.0

def _wf_layout():
    off = {}
    aq, ak, av = 0, 512, 1024
    bq, bkv = 1536, 1792
    cq, ck, cv = 2080, 2592, 2720
    dq, dk, dv = 2848, 3360, 3488
    g0 = 3616
    cols = []
    def add(name, idx):
        off[name] = sum(len(c) for c in cols)
        cols.append(np.asarray(idx, dtype=np.int64))
    def sw(idx):
        idx = np.asarray(idx)
        return np.where(idx >= 0, idx ^ 1, -1)
    r = np.arange
    for h in range(4):
        add(f"AQ{h}", aq + h * 128 + r(128)); add(f"AQ{h}s", sw(aq + h * 128 + r(128)))
    for h in range(4):
        add(f"AK{h}", ak + h * 128 + r(128)); add(f"AK{h}s", sw(ak + h * 128 + r(128)))
    for m in range(4):
        idx = np.concatenate([cq + m * 64 + r(64), cq + (4 + m) * 64 + r(64)])
        add(f"CQ{m}", idx); add(f"CQ{m}s", sw(idx))
    add("CK", ck + r(128)); add("CKs", sw(ck + r(128)))
    for m in range(4):
        idx = np.concatenate([dq + m * 64 + r(64), dq + (4 + m) * 64 + r(64)])
        add(f"DQ{m}", idx); add(f"DQ{m}s", sw(idx))
    add("DK", dk + r(128)); add("DKs", sw(dk + r(128)))
    add("BQ0", bq + r(128)); add("BQ1", bq + 128 + r(128))
    add("BC0", bkv + r(128)); add("BC1", bkv + 128 + r(128))
    kr = np.concatenate([np.full(64, -1), bkv + 256 + r(32), np.full(32, -1)])
    add("KR", kr); add("KRs", sw(kr))
    for k in range(4):
        for j in range(8):
            add(f"G{k*8+j}", g0 + k * 1024 + j * 128 + r(128))
    add("AV", av + r(512))
    add("CDV", np.concatenate([cv + r(128), dv + r(128)]))
    allc = np.concatenate(cols)
    return off, allc

WF_OFF, WF_COLS = _wf_layout()
NWF = len(WF_COLS)


class _Rec:
    def __getattr__(self, name):
        def f(*a, **k):
            self.call = (name, a, k)
            return self
        return f


def _cap(fn):
    r = _Rec()
    fn(r)
    name, a, k = r.call
    return lambda E: getattr(E, name)(*a, **k)


class Sched:
    CE = ("pe", "act", "dve", "pool")
    ND = 12

    def __init__(self, nc):
        self.nc = nc
        self.prog = {e: [] for e in ("pe", "act", "dve", "pool", "sp")}
        self.cnt = {e: 0 for e in self.CE}
        self.sem = {e: nc.alloc_semaphore(f"sem_{e}") for e in self.CE}
        self.waited = {}
        self.lastw = {}
        self.rds = {}
        self.dsem = {q: [nc.alloc_semaphore(f"dsem_{q}{i}") for i in range(self.ND)] for q in ("sp", "pool")}
        self.duse = {q: [0] * self.ND for q in ("sp", "pool")}
        self.dnext = {"sp": 0, "pool": 0}
        self.nsem = 0
        self.cctoks = []

    def _deps(self, reads, writes):
        deps = []
        for k in reads:
            t = self.lastw.get(k)
            if t is not None:
                deps.append(t)
        for k in writes:
            t = self.lastw.get(k)
            if t is not None:
                deps.append(t)
            r = self.rds.get(k)
            if r:
                deps.extend(r["c"].values())
                deps.extend(r["d"])
        return deps

    def _mark(self, tok, reads, writes):
        for k in reads:
            r = self.rds.setdefault(k, {"c": {}, "d": []})
            if tok[0] == "c":
                r["c"][tok[1]] = tok
            else:
                r["d"].append(tok)
        for k in writes:
            self.lastw[k] = tok
            self.rds[k] = {"c": {}, "d": []}

    def _waits(self, eng, deps):
        p = self.prog[eng]
        for t in deps:
            if t[0] == "c":
                _, e2, c = t
                if e2 == eng and eng == "pe":
                    continue
                key = (eng, e2)
                if self.waited.get(key, 0) >= c:
                    continue
                self.waited[key] = c
                s = self.sem[e2]
                p.append(lambda E, s=s, c=c: E.wait_ge(s, c))
            else:
                _, s, v, sid = t
                key = (eng, sid)
                if self.waited.get(key, 0) >= v:
                    continue
                self.waited[key] = v
                p.append(lambda E, s=s, v=v: E.wait_ge(s, v))

    def op(self, eng, fn, reads=(), writes=()):
        self._waits(eng, self._deps(reads, writes))
        self.cnt[eng] += 1
        s = self.sem[eng]
        fn = _cap(fn)
        self.prog[eng].append(lambda E, fn=fn, s=s: fn(E).then_inc(s, 1))
        tok = ("c", eng, self.cnt[eng])
        self._mark(tok, reads, writes)
        return tok

    def dma(self, q, fn, reads=(), writes=()):
        self._waits(q, self._deps(reads, writes))
        i = self.dnext[q] % self.ND
        self.dnext[q] += 1
        s = self.dsem[q][i]
        sid = (q, i)
        if self.duse[q][i] > 0:
            self._waits(q, [("d", s, 16 * self.duse[q][i], sid)])
        self.duse[q][i] += 1
        fn = _cap(fn)
        self.prog[q].append(lambda E, fn=fn, s=s: fn(E).then_inc(s, 16))
        tok = ("d", s, 16 * self.duse[q][i], sid)
        self._mark(tok, reads, writes)
        return tok

    def collective(self, fn, reads=(), writes=()):
        self._waits("pool", self._deps(reads, writes))
        s = self.nc.alloc_semaphore(f"ccsem{self.nsem}")
        self.nsem += 1
        fn = _cap(fn)
        self.prog["pool"].append(lambda E, fn=fn, s=s: fn(E).then_inc(s))
        tok = ("d", s, 1, ("cc", self.nsem))
        self.cctoks.append(tok)
        self._mark(tok, reads, writes)
        return tok

    def barrier(self):
        toks = [("c", e, self.cnt[e]) for e in self.CE if self.cnt[e] > 0]
        for q in ("sp", "pool"):
            for i in range(self.ND):
                if self.duse[q][i] > 0:
                    toks.append(("d", self.dsem[q][i], 16 * self.duse[q][i], (q, i)))
        toks += self.cctoks
        for e in ("pe", "act", "dve", "pool", "sp"):
            self._waits(e, toks)

    def wait_all(self, eng, keys):
        self._waits(eng, self._deps(keys, ()))

    def emit(self):
        nc = self.nc
        with nc.Block() as block:
            @block.tensor
            def _(E):
                for f in self.prog["pe"]:
                    f(E)

            @block.scalar
            def _(E):
                for f in self.prog["act"]:
                    f(E)

            @block.vector
            def _(E):
                for f in self.prog["dve"]:
                    f(E)

            @block.gpsimd
            def _(E):
                for f in self.prog["pool"]:
                    f(E)

            @block.sync
            def _(E):
                for f in self.prog["sp"]:
                    f(E)


class Builder:
    def __init__(self, nlayers=NL, dbg=None, stop=None):
        self.stop = stop
        self.nl = nlayers
        self.dbg = dbg or []
        nc = bass.Bass("TRN2", target_bir_lowering=False)
        self.nc = nc
        self.S = Sched(nc)
        self.uid = 0
        self._decl_dram()
        self._alloc_sbuf()
        self.build()
        self.S.emit()

    def _decl_dram(self):
        nc = self.nc
        I = lambda n, s, dt=F32: nc.dram_tensor(n, s, dt, kind="ExternalInput").ap()
        self.x_in = I("x", [NT, D])
        self.ctx_in = I("ctx", [NC, D])
        self.cT = I("cT", [128, 8, 2])
        self.wmod = I("w_mod", [NL, D, 6 * D])
        self.bmodT = I("bmodT", [NL, 128, 48, 2])
        self.wf = I("wf", [NL, D, NWF])
        self.wuq = I("wuq", [NL, 256, 768])
        self.wuqs = I("wuqs", [NL, 256, 768])
        self.wkn = I("wkn", [NL, 256, 512])
        self.wvb = I("wvb", [NL, 256, 512])
        self.wbr = I("w_branch", [NL, 4, 512, D])
        self.wout = I("w_out", [NL, D, D])
        self.wup = I("w_up", [NL, D, 2 * DFF])
        self.wdn = I("w_down", [NL, DFF, D])
        self.pp = I("pp", [128, NPP])
        self.rope64 = I("rope64", [2, 128, TOK])
        self.rope32 = I("rope32", [2, 96, TOK])
        self.dmask = I("dmask", [10, 128, 512])
        self.hflag = I("hflag", [2, 128, 4, 8])
        self.consts = I("consts", [4, 128, 128])
        self.y_out = nc.dram_tensor("y", [NT, D], F32, kind="ExternalOutput").ap()
        T = lambda n, s, dt=BF16: nc.dram_tensor(n, s, dt).ap()
        self.cg = {"KA0": (256, NT), "KA1": (256, NT), "KB0": (192, NT), "KB1": (192, NT), "KB2": (192, NT), "KB3": (192, NT),
                   "KCD": (256, NT), "VA0": (NT, 256), "VA1": (NT, 256), "VB0": (NT, 256), "VB1": (NT, 256), "VCD": (NT, 256)}
        self.sendg = {g: T("send_" + g, [r, c]) for g, (r, c) in self.cg.items()}
        self.recvg = {g: T("recv_" + g, [4 * r, c]) for g, (r, c) in self.cg.items()}
        self.ktc = T("ktc", [KTROWS, NC])
        self.vc = T("vc", [NC, VW])
        self.kdl = T("kdl", [128, NT])
        self.vdl = T("vdl", [NT, 128])
        self.qs = T("qs", [12 * 128 + 8 * 96, TOK])
        self.gs = T("gs", [32 * 128, TOK])
        self.os_ = T("os", [16 * 128, TOK])
        self.send2 = T("send2", [128, 16])
        self.recv2 = T("recv2", [4 * 128, 16])
        self.dbg_out = {}
        for name, shape, dt in self.dbg:
            self.dbg_out[name] = nc.dram_tensor("dbg_" + name, shape, F32, kind="ExternalOutput").ap()

    def _alloc_sbuf(self):
        nc = self.nc
        self.soff = 16512
        def P(name, shape, dt):
            nb = int(np.prod(shape[1:])) * (4 if dt == F32 else 2)
            nb = (nb + 31) // 32 * 32
            t = nc.alloc_sbuf_tensor_at(name, shape, dt, offset=self.soff)
            self.soff += nb
            return t.ap()
        self.xT = P("xT", [128, 8, TOK], F32)
        self.modT = P("modT", [128, NL, 48, 2], F32)
        self.ident_f = P("ident_f", [128, 128], F32)
        self.ones_f = P("ones_f", [128, 128], F32)
        self.blk_f = P("blk_f", [128, 128], F32)
        self.shift_f = P("shift_f", [128, 128], F32)
        self.ident_b = P("ident_b", [128, 128], BF16)
        self.ones_b = P("ones_b", [128, 128], BF16)
        self.ppt = P("ppt", [128, NPP], F32)
        self.lamt = P("lamt", [128, NL, 8], F32)
        self.sinkrow = P("sinkrow", [128, 2, 512], F32)
        self.scT = P("scT", [128, 8, 2], F32)
        self.arena0 = self.soff
        assert self.arena0 < 105000, self.arena0

    def A(self, name, shape, dt, off):
        self.uid += 1
        nb = int(np.prod(shape[1:])) * (4 if dt == F32 else 2)
        assert self.arena0 + off + nb <= 229344, (name, self.arena0 + off + nb)
        return self.nc.alloc_sbuf_tensor_at(f"{name}_{self.uid}", shape, dt, offset=self.arena0 + off).ap()

    def dump(self, name, src_ap, keys):
        if name in self.dbg_out:
            o = self.dbg_out[name]
            if len(src_ap.shape) == 1:
                src_ap = src_ap.rearrange("(r c) -> r c", c=2048)
            if len(src_ap.shape) == 3:
                views = [(o[:, a, :], src_ap[:, a, :]) for a in range(src_ap.shape[1])]
            else:
                views = [(o, src_ap)]
            i = 0
            for (ov, sv) in views:
                ncol = sv.shape[1]
                for c0 in range(0, ncol, 1024):
                    c1 = min(ncol, c0 + 1024)
                    key = ("dbg", name, i)
                    i += 1
                    self.S.dma("pool", lambda E: E.dma_start(out=ov[:, c0:c1], in_=sv[:, c0:c1]), reads=keys, writes=[key])
                    self.outkeys.append(key)

    def build(self):
        S = self.S
        nc = self.nc
        self.outkeys = []
        self.psall = nc.alloc_psum_tensor("psall", [128, 4096], F32).ap()
        self.ps = [self.psall[:, i * 512:(i + 1) * 512] for i in range(8)]
        self.phase0()
        for l in range(self.nl):
            self.layer(l)
        S.barrier()
        self.final()
        S.wait_all("sp", self.outkeys)

    def phase0(self):
        S = self.S
        cst = [self.ident_f, self.ones_f, self.blk_f, self.shift_f]
        for i, t in enumerate(cst):
            S.dma("sp", lambda E, i=i, t=t: E.dma_start(out=t, in_=self.consts[i]), writes=[("c", i)])
        S.dma("pool", lambda E: E.dma_start(out=self.ident_b, in_=self.consts[0]), writes=[("c", "ib")])
        S.dma("pool", lambda E: E.dma_start(out=self.ones_b, in_=self.consts[1]), writes=[("c", "ob")])
        S.dma("sp", lambda E: E.dma_start(out=self.ppt, in_=self.pp), writes=["pp"])
        S.dma("sp", lambda E: E.dma_start(out=self.scT, in_=self.cT), writes=["scT"])
        S.op("act", lambda E: E.activation(out=self.scT, in_=self.scT, func=AF.Silu), reads=["scT"], writes=["scT"])
        xst = [self.A("xst", [128, D], F32, i * 4096) for i in range(2)]
        for tt in range(18):
            st = xst[tt % 2]
            src = self.x_in[tt * 128:(tt + 1) * 128, :] if tt < 16 else self.ctx_in[(tt - 16) * 128:(tt - 15) * 128, :]
            S.dma("sp", lambda E, st=st, src=src: E.dma_start(out=st, in_=src), writes=[("xst", tt % 2)])
            for half in range(2):
                bank = (tt * 2 + half) % 4
                ps = self.ps[bank]
                for q in range(4):
                    fc = half * 4 + q
                    S.op("pe", lambda E, ps=ps, q=q, st=st, fc=fc: E.transpose(
                        out=ps[:, q * 128:(q + 1) * 128], in_=st[:, fc * 128:(fc + 1) * 128], identity=self.ident_f),
                        reads=[("xst", tt % 2), ("c", 0)], writes=[("ps", bank)])
                dst = self.xT[:, half * 4:(half + 1) * 4, tt * 128:(tt + 1) * 128]
                src2 = ps.rearrange("p (a b) -> p a b", a=4)
                eng = "act" if half == 0 else "dve"
                if eng == "act":
                    S.op("act", lambda E, dst=dst, src2=src2: E.activation(out=dst, in_=src2, func=AF.Identity),
                         reads=[("ps", bank)], writes=[("xT", c) for c in range(5)])
                else:
                    S.op("dve", lambda E, dst=dst, src2=src2: E.tensor_copy(out=dst, in_=src2),
                         reads=[("ps", bank)], writes=[("xT", c) for c in range(5)])
        wm = [self.A("wm", [128, 8, 512], F32, 8192 + i * 16384) for i in range(2)]
        bm = self.A("bm", [128, 48, 2], F32, 8192 + 2 * 16384)
        n = 0
        for l in range(self.nl):
            S.dma("sp", lambda E, l=l: E.dma_start(out=bm, in_=self.bmodT[l]), writes=["bm"])
            for blk in range(12):
                w = wm[n % 2]
                S.dma("sp" if n % 2 == 0 else "pool", lambda E, w=w, l=l, blk=blk: E.dma_start(
                    out=w, in_=self.wmod[l, :, blk * 512:(blk + 1) * 512].rearrange("(kc p) c -> p kc c", p=128)),
                    writes=[("wm", n % 2)])
                bank = 4 + n % 2
                ps = self.ps[bank]
                for c4 in range(4):
                    for kc in range(8):
                        S.op("pe", lambda E, ps=ps, c4=c4, kc=kc, w=w: E.matmul(
                            ps[:, c4 * 2:c4 * 2 + 2], lhsT=w[:, kc, c4 * 128:(c4 + 1) * 128], rhs=self.scT[:, kc, :],
                            start=(kc == 0), stop=(kc == 7)),
                            reads=[("wm", n % 2), "scT"], writes=[("ps", bank)])
                dst = self.modT[:, l, blk * 4:(blk + 1) * 4, :]
                S.op("dve", lambda E, dst=dst, ps=ps, blk=blk: E.tensor_tensor(
                    out=dst, in0=ps[:, 0:8].rearrange("p (a b) -> p a b", a=4), in1=bm[:, blk * 4:(blk + 1) * 4, :], op=ALU.add),
                    reads=[("ps", bank), "bm"], writes=["modT"])
                n += 1
            for kind in (1, 4):
                dst = self.modT[:, l, kind * 8:(kind + 1) * 8, :]
                S.op("dve", lambda E, dst=dst: E.tensor_scalar(out=dst, in0=dst, scalar1=1.0, scalar2=None, op0=ALU.add),
                     reads=["modT"], writes=["modT"])

    def mod(self, l, kind, chunk, j):
        return self.modT[:, l, kind * 8 + chunk, j:j + 1]

    def rstd_cols(self, src_fn, nchunks, n, inv_dim, sq_t, rstd_t, bank, keys_r, lhs=None, lhs_key=("c", 1)):
        S = self.S
        ps = self.ps[bank]
        lhs = self.ones_f if lhs is None else lhs
        for kc in range(nchunks):
            sq = sq_t[kc % len(sq_t)]
            sqk = ("sq", id(sq_t), kc % len(sq_t))
            S.op("act", lambda E, sq=sq, kc=kc: E.activation(out=sq[:, :n], in_=src_fn(kc), func=AF.Square),
                 reads=keys_r, writes=[sqk])
            S.op("pe", lambda E, sq=sq, kc=kc: E.matmul(ps[:, :n], lhsT=lhs, rhs=sq[:, :n], start=(kc == 0), stop=(kc == nchunks - 1)),
                 reads=[sqk, lhs_key], writes=[("ps", bank)])
        rk = ("rstd", id(rstd_t))
        S.op("act", lambda E: E.activation(out=rstd_t[:, :n], in_=ps[:, :n], func=AF.Sqrt, scale=inv_dim, bias=self.ppt[:, PP_EPS:PP_EPS + 1]),
             reads=[("ps", bank), "pp"], writes=[rk])
        S.op("dve", lambda E: E.reciprocal(out=rstd_t[:, :n], in_=rstd_t[:, :n]), reads=[rk], writes=[rk])
        return rk

    def final(self):
        S = self.S
        sq_t = [self.A("fsq", [128, 512], F32, i * 2048) for i in range(2)]
        rstd = self.A("frs", [128, 512], F32, 4096)
        yT = self.A("fy", [128, 8, 512], F32, 8192)
        ost = [self.A("fo", [128, D], F32, 8192 + 16384 + i * 4096) for i in range(2)]
        for ci in range(4):
            t0, n = CH[ci]
            rk = self.rstd_cols(lambda kc: self.xT[:, kc, t0:t0 + n], 8, n, 1.0 / D, sq_t, rstd, 4, [("xT", ci)])
            for kc in range(8):
                S.op("dve", lambda E, kc=kc: E.scalar_tensor_tensor(
                    out=yT[:, kc, :n], in0=self.xT[:, kc, t0:t0 + n], scalar=self.ppt[:, PP_FN + kc:PP_FN + kc + 1],
                    in1=rstd[:, :n], op0=ALU.mult, op1=ALU.mult),
                    reads=[("xT", ci), rk, "pp"], writes=[("fy", kc)])
            for ti in range(4):
                tt = ci * 4 + ti
                o = ost[tt % 2]
                for half in range(2):
                    bank = (tt * 2 + half) % 4
                    ps = self.ps[bank]
                    for q in range(4):
                        kc = half * 4 + q
                        S.op("pe", lambda E, ps=ps, q=q, kc=kc, ti=ti: E.transpose(
                            out=ps[:, q * 128:(q + 1) * 128], in_=yT[:, kc, ti * 128:(ti + 1) * 128], identity=self.ident_f),
                            reads=[("fy", kc), ("c", 0)], writes=[("ps", bank)])
                    dst = o[:, half * 512:(half + 1) * 512]
                    if half == 0:
                        S.op("act", lambda E, dst=dst, ps=ps: E.activation(out=dst, in_=ps, func=AF.Identity),
                             reads=[("ps", bank)], writes=[("fo", tt % 2, half)])
                    else:
                        S.op("dve", lambda E, dst=dst, ps=ps: E.tensor_copy(out=dst, in_=ps),
                             reads=[("ps", bank)], writes=[("fo", tt % 2, half)])
                S.dma("sp", lambda E, o=o, tt=tt: E.dma_start(out=self.y_out[tt * 128:(tt + 1) * 128, :], in_=o),
                      reads=[("fo", tt % 2, 0), ("fo", tt % 2, 1)], writes=[("y", tt)])
                self.outkeys.append(("y", tt))

    def layer(self, l):
        last = (l == NL - 1)
        self.S.barrier()
        self.prep_layer(l)
        self.p1(l, last)
        if self.stop in ("p1a", "p1k", "p1v", "p1b", "p1g"):
            return
        self.dump(f"qs{l}", self.qs, [("qs", r0, ci) for r0 in list(range(0, 1536, 128)) + [1536 + h * 96 for h in range(8)] for ci in (range(4) if last else range(5))])
        if self.stop == "p1":
            return
        self.S.barrier()
        self.p2(l, last)
        self.dump(f"os{l}", self.os_, [k for ci in (range(4) if last else range(5)) for k in self.os_keys(ci)])
        if self.stop == "p2":
            return
        self.S.barrier()
        self.p3(l, last)
        if self.stop == "p3":
            return
        self.S.barrier()
        self.p4(l, last)

    def ppc(self, l, name, i=0):
        c = PL0 + l * PLN + PLO[name] + i
        return self.ppt[:, c:c + 1]

    def prep_layer(self, l):
        S = self.S
        lam = self.lamt
        base = PL0 + l * PLN + PLO["lam"]
        tmp = self.A("lamtmp", [128, 64], F32, 0)
        for i in range(2):
            S.op("dve", lambda E: E.tensor_tensor(out=tmp, in0=self.ppt[:, base + i * 128:base + i * 128 + 64],
                                                  in1=self.ppt[:, base + i * 128 + 64:base + i * 128 + 128], op=ALU.mult),
                 reads=["pp"], writes=["lamtmp"])
            S.op("dve", lambda E: E.tensor_reduce(out=lam[:, l, i:i + 1], in_=tmp, axis=mybir.AxisListType.X, op=ALU.add),
                 reads=["lamtmp"], writes=["lamt"])
        S.op("act", lambda E: E.activation(out=lam[:, l, 0:2], in_=lam[:, l, 0:2], func=AF.Exp), reads=["lamt"], writes=["lamt"])
        S.op("dve", lambda E: E.tensor_tensor(out=lam[:, l, 2:3], in0=lam[:, l, 0:1], in1=lam[:, l, 1:2], op=ALU.subtract),
             reads=["lamt"], writes=["lamt"])
        lam_init = 0.8 - 0.6 * math.exp(-0.3 * l)
        S.op("dve", lambda E: E.tensor_scalar(out=lam[:, l, 3:4], in0=lam[:, l, 2:3], scalar1=-1.0, scalar2=-lam_init,
                                              op0=ALU.mult, op1=ALU.add), reads=["lamt"], writes=["lamt"])
        sb = PL0 + l * PLN + PLO["sink"]
        sk = self.A("sinkexp", [128, 8], F32, 512)
        S.op("act", lambda E: E.activation(out=sk, in_=self.ppt[:, sb:sb + 8], func=AF.Exp), reads=["pp"], writes=["sinkexp"])
        for j in range(2):
            for g in range(4):
                S.op("dve", lambda E: E.tensor_scalar(out=self.sinkrow[:, j, g * 128:(g + 1) * 128], in0=self.ones_f,
                                                      scalar1=sk[:, 4 * j + g:4 * j + g + 1], scalar2=None, op0=ALU.mult),
                     reads=["sinkexp", ("c", 1)], writes=["sinkrow"])

    def modulate(self, l, ksh, ksc, dst, dcol, dkey, chunks, o_sq, o_rstd, o_tmp):
        S = self.S
        sq_t = [self.A("msq", [128, 512], F32, o_sq + i * 2048) for i in range(2)]
        rstd = self.A("mrs", [128, 512], F32, o_rstd)
        tmp = [self.A("mtmp", [128, 512], F32, o_tmp + i * 2048) for i in range(2)]
        for ci in chunks:
            t0, n = CH[ci]
            j = 1 if ci == 4 else 0
            rk = self.rstd_cols(lambda kc: self.xT[:, kc, t0:t0 + n], 8, n, 1.0 / D, sq_t, rstd, 7, [("xT", ci)])
            d0 = dcol(ci)
            for kc in range(8):
                t = tmp[kc % 2]
                tk = ("mtmp", o_tmp, kc % 2)
                S.op("dve", lambda E: E.tensor_tensor(out=t[:, :n], in0=self.xT[:, kc, t0:t0 + n], in1=rstd[:, :n], op=ALU.mult),
                     reads=[("xT", ci), rk], writes=[tk])
                S.op("act", lambda E: E.activation(out=dst[:, kc, d0:d0 + n], in_=t[:, :n], func=AF.Identity,
                                                   scale=self.mod(l, ksc, kc, j), bias=self.mod(l, ksh, kc, j)),
                     reads=[tk, "modT"], writes=[(dkey, ci)])

    def p1(self, l, last):
        S = self.S
        hT = self.A("hT", [128, 8, TOK], BF16, 0)
        ropeC = self.A("ropeC", [128, TOK], F32, 36864)
        ropeS = self.A("ropeS", [128, TOK], F32, 46080)
        wt = [self.A("wt", [128, 8, 256], BF16, 55296 + i * 4096) for i in range(2)]
        T = [self.A("p1t", [128, 512], F32, 63488 + i * 2048) for i in range(6)]
        sqt = [self.A("p1sq", [128, 512], F32, 75776 + i * 2048) for i in range(2)]
        rst = self.A("p1rs", [128, 512], F32, 79872)
        lat = self.A("lat", [128, 4, 512], F32, 81920)
        latn = self.A("latn", [128, 4, 512], BF16, 90112)
        ost = [self.A("p1o", [128, 512], BF16, 94208 + i * 1024) for i in range(4)]
        wuq = self.A("wuq", [128, 2, 768], BF16, 98304)
        wuqs = self.A("wuqs", [128, 2, 768], BF16, 101376)
        wkn = self.A("wkn", [128, 2, 512], BF16, 104448)
        wvb = self.A("wvb", [128, 2, 512], BF16, 106496)
        KRt = self.A("KRt", [128, 512], BF16, 108544)
        wv = self.A("wv", [128, 8, 512], BF16, 109568)
        rB = [self.A("rB", [128, 512], F32, 117760 + i * 2048) for i in range(2)]
        HK = [("hT", ci) for ci in range(5)]

        S.dma("sp", lambda E: E.dma_start(out=ropeC, in_=self.rope64[0]), writes=["ropeC"])
        S.dma("sp", lambda E: E.dma_start(out=ropeS, in_=self.rope64[1]), writes=["ropeS"])
        for t, src, k in ((wuq, self.wuq, "wuq"), (wuqs, self.wuqs, "wuqs"), (wkn, self.wkn, "wkn"), (wvb, self.wvb, "wvb")):
            S.dma("pool", lambda E: E.dma_start(out=t, in_=src[l].rearrange("(b p) c -> p b c", p=128)), writes=[k])
        self.modulate(l, 0, 1, hT, lambda ci: CH[ci][0], "hT", range(5), 75776, 79872, 63488)
        self.dump(f"hT{l}", hT, HK)
        S.barrier()
        if self.stop == "p1a":
            return

        self.sendkeys = {g: [] for g in self.cg}
        cnt = [0]
        wcnt = [0]
        ocnt = [0]

        gq = []

        def gather_later(g):
            gq.append([2, g])

        def gq_tick(force=False):
            for it in gq:
                it[0] -= 1
            while gq and (force or gq[0][0] <= 0):
                self.gather(gq.pop(0)[1])

        def load_w(name, ncols=256):
            gq_tick()
            i = wcnt[0] % 2
            wcnt[0] += 1
            o = WF_OFF[name]
            S.dma("pool", lambda E: E.dma_start(out=wt[i][:, :, 0:ncols],
                                                in_=self.wf[l, :, o:o + ncols].rearrange("(kc p) c -> p kc c", p=128)),
                  writes=[("wt", i)])
            return wt[i], ("wt", i)

        def mm8(bank, w, wk, c0, ncols, ci, nrows=128):
            t0, n = CH[ci]
            ps = self.ps[bank]
            for kc in range(8):
                S.op("pe", lambda E: E.matmul(ps[0:nrows, :n], lhsT=w[:, kc, c0:c0 + nrows], rhs=hT[:, kc, t0:t0 + n],
                                              start=(kc == 0), stop=(kc == 7)),
                     reads=[wk, ("hT", ci)], writes=[("ps", bank)])

        def store(o, ok, nrows, n, dst, dkeys, q="sp"):
            S.dma(q, lambda E: E.dma_start(out=dst, in_=o[0:nrows, :n]), reads=[ok], writes=dkeys)

        def rope_block(name, ci, dsts, norm=None):
            w, wk = name
            t0, n = CH[ci]
            c = cnt[0]
            cnt[0] += 1
            ba, bb = (c % 2) * 2, (c % 2) * 2 + 1
            mm8(ba, w, wk, 0, 128, ci)
            mm8(bb, w, wk, 128, 128, ci)
            pa, pb = self.ps[ba], self.ps[bb]
            u1, u2 = T[(c % 2) * 3], T[(c % 2) * 3 + 1]
            k1, k2 = ("p1t", (c % 2) * 3), ("p1t", (c % 2) * 3 + 1)
            oi = ocnt[0] % 4
            ocnt[0] += 1
            o, ok = ost[oi], ("p1o", oi)
            if norm is None:
                S.op("dve", lambda E: E.tensor_tensor(out=u1[:, :n], in0=pa[:, :n], in1=ropeC[:, t0:t0 + n], op=ALU.mult),
                     reads=[("ps", ba), "ropeC"], writes=[k1])
                S.op("dve", lambda E: E.tensor_tensor(out=u2[:, :n], in0=pb[:, :n], in1=ropeS[:, t0:t0 + n], op=ALU.mult),
                     reads=[("ps", bb), "ropeS"], writes=[k2])
                S.op("dve", lambda E: E.tensor_tensor(out=o[:, :n], in0=u1[:, :n], in1=u2[:, :n], op=ALU.add),
                     reads=[k1, k2], writes=[ok])
            else:
                g, gs = norm
                sq = T[(c % 2) * 3 + 2]
                k3 = ("p1t", (c % 2) * 3 + 2)
                bn = 4 + c % 2
                S.op("act", lambda E: E.activation(out=sq[:, :n], in_=pa[:, :n], func=AF.Square), reads=[("ps", ba)], writes=[k3])
                S.op("pe", lambda E: E.matmul(self.ps[bn][:, :n], lhsT=self.blk_f, rhs=sq[:, :n], start=True, stop=True),
                     reads=[k3, ("c", 2)], writes=[("ps", bn)])
                S.op("act", lambda E: E.activation(out=sq[:, :n], in_=self.ps[bn][:, :n], func=AF.Sqrt, scale=1.0 / 64,
                                                   bias=self.ppt[:, PP_EPS:PP_EPS + 1]), reads=[("ps", bn), "pp"], writes=[k3])
                S.op("dve", lambda E: E.reciprocal(out=sq[:, :n], in_=sq[:, :n]), reads=[k3], writes=[k3])
                S.op("dve", lambda E: E.scalar_tensor_tensor(out=u1[:, :n], in0=pa[:, :n], scalar=g, in1=ropeC[:, t0:t0 + n],
                                                             op0=ALU.mult, op1=ALU.mult), reads=[("ps", ba), "ropeC", "pp"], writes=[k1])
                S.op("dve", lambda E: E.scalar_tensor_tensor(out=u2[:, :n], in0=pb[:, :n], scalar=gs, in1=ropeS[:, t0:t0 + n],
                                                             op0=ALU.mult, op1=ALU.mult), reads=[("ps", bb), "ropeS", "pp"], writes=[k2])
                S.op("dve", lambda E: E.tensor_tensor(out=u1[:, :n], in0=u1[:, :n], in1=u2[:, :n], op=ALU.add),
                     reads=[k1, k2], writes=[k1])
                S.op("dve", lambda E: E.tensor_tensor(out=o[:, :n], in0=u1[:, :n], in1=sq[:, :n], op=ALU.mult),
                     reads=[k1, k3], writes=[ok])
            for dst, dk in dsts:
                store(o, ok, 128, n, dst, dk)

        def kdst(row0, ci, extra=None):
            t0, n = CH[ci]
            if ci < 4:
                g, lr = self.kt_loc(row0)
                d = [(self.sendg[g][lr:lr + 128, t0:t0 + n], [("send", row0, ci)])]
                self.sendkeys[g].append(("send", row0, ci))
                if extra is not None:
                    d.append((extra[:, t0:t0 + n], [("kdl", ci)]))
            else:
                d = [(self.ktc[row0:row0 + 128, :], [("ktc", row0)])]
            return d

        for h in range(4):
            w = load_w(f"AK{h}")
            for ci in range(5):
                rope_block(w, ci, kdst(h * 128, ci))
        w = load_w("CK")
        for ci in range(5):
            rope_block(w, ci, kdst(1280, ci), norm=(self.ppc(l, "ckg"), self.ppc(l, "ckgs")))
        w = load_w("DK")
        for ci in range(5):
            rope_block(w, ci, kdst(1408, ci, extra=self.kdl))
        if self.stop == "p1k":
            return

        vst = [self.A("vst", [128, 512], BF16, 121856 + i * 1024) for i in range(2)]
        vcnt = [0]

        def vproj(ncols, wcol_name, dcol0, also_vdl=False):
            o = WF_OFF[wcol_name]
            S.dma("pool", lambda E: E.dma_start(out=wv[:, :, 0:ncols], in_=self.wf[l, :, o:o + ncols].rearrange("(kc p) c -> p kc c", p=128)),
                  writes=["wv"])
            for tt in range(18):
                ci = min(tt // 4, 4)
                c = vcnt[0]
                vcnt[0] += 1
                bank = 4 + c % 2
                ps = self.ps[bank]
                for kc in range(8):
                    S.op("pe", lambda E: E.matmul(ps[:, :ncols], lhsT=hT[:, kc, tt * 128:(tt + 1) * 128], rhs=wv[:, kc, 0:ncols],
                                                  start=(kc == 0), stop=(kc == 7)), reads=["wv", ("hT", ci)], writes=[("ps", bank)])
                v = vst[c % 2]
                vk = ("vst", c % 2)
                if c % 2 == 0:
                    S.op("act", lambda E: E.activation(out=v[:, :ncols], in_=ps[:, :ncols], func=AF.Identity), reads=[("ps", bank)], writes=[vk])
                else:
                    S.op("dve", lambda E: E.tensor_copy(out=v[:, :ncols], in_=ps[:, :ncols]), reads=[("ps", bank)], writes=[vk])
                if tt < 16:
                    for cc0 in range(0, ncols, 256):
                        g, lc = self.v_loc(dcol0 + cc0)
                        key = ("sendv", g, tt)
                        self.sendkeys[g].append(key)
                        S.dma("sp", lambda E: E.dma_start(out=self.sendg[g][tt * 128:(tt + 1) * 128, lc:lc + 256], in_=v[:, cc0:cc0 + 256]),
                              reads=[vk], writes=[key])
                    if also_vdl:
                        S.dma("sp", lambda E: E.dma_start(out=self.vdl[tt * 128:(tt + 1) * 128, :], in_=v[:, 128:256]),
                              reads=[vk], writes=[("vdl", tt)])
                else:
                    S.dma("sp", lambda E: E.dma_start(out=self.vc[(tt - 16) * 128:(tt - 15) * 128, dcol0:dcol0 + ncols], in_=v[:, :ncols]),
                          reads=[vk], writes=[("vc", wcol_name, tt)])
        self.late = ["KB0", "KB1", "KB2", "KB3", "VB0", "VB1", "KCD", "VCD"]
        vproj(512, "AV", 0)
        vproj(256, "CDV", 1024, also_vdl=True)
        if self.stop == "p1v":
            return

        wBC, wBCk = load_w("BC0")
        wKR, wKRk = load_w("KR")

        def load_rope32(ci):
            t0, n = CH[ci]
            for i in range(2):
                S.dma("sp", lambda E: E.dma_start(out=rB[i][0:96, :n], in_=self.rope32[i, :, t0:t0 + n]), writes=[("rB", i)])

        def latent_norm(w, wk, ci, li0, gname):
            t0, n = CH[ci]
            mm8(0, w, wk, 0, 128, ci)
            mm8(1, w, wk, 128, 128, ci)
            S.op("act", lambda E: E.activation(out=lat[:, li0, :n], in_=self.ps[0][:, :n], func=AF.Identity),
                 reads=[("ps", 0)], writes=[("lat", li0)])
            S.op("dve", lambda E: E.tensor_copy(out=lat[:, li0 + 1, :n], in_=self.ps[1][:, :n]), reads=[("ps", 1)], writes=[("lat", li0 + 1)])
            rk = self.rstd_cols(lambda kc: lat[:, li0 + kc, :n], 2, n, 1.0 / 256, sqt, rst, 6, [("lat", li0), ("lat", li0 + 1)])
            for b in range(2):
                S.op("dve", lambda E: E.scalar_tensor_tensor(out=latn[:, li0 + b, :n], in0=lat[:, li0 + b, :n], scalar=self.ppc(l, gname, b),
                                                             in1=rst[:, :n], op0=ALU.mult, op1=ALU.mult),
                     reads=[("lat", li0 + b), rk, "pp"], writes=[("latn", li0 + b)])

        send_kB = 512
        for ci in range(5):
            t0, n = CH[ci]
            load_rope32(ci)
            latent_norm(wBC, wBCk, ci, 2, "kvn")
            mm8(2, wKR, wKRk, 0, 128, ci)
            mm8(3, wKR, wKRk, 128, 128, ci)
            S.op("dve", lambda E: E.tensor_tensor(out=T[0][0:96, :n], in0=self.ps[2][0:96, :n], in1=rB[0][0:96, :n], op=ALU.mult),
                 reads=[("ps", 2), ("rB", 0)], writes=[("p1t", 0)])
            S.op("dve", lambda E: E.tensor_tensor(out=T[1][0:96, :n], in0=self.ps[3][0:96, :n], in1=rB[1][0:96, :n], op=ALU.mult),
                 reads=[("ps", 3), ("rB", 1)], writes=[("p1t", 1)])
            S.op("dve", lambda E: E.tensor_tensor(out=KRt[0:96, :n], in0=T[0][0:96, :n], in1=T[1][0:96, :n], op=ALU.add),
                 reads=[("p1t", 0), ("p1t", 1)], writes=["KRt"])
            for h in range(8):
                bank = 4 + h % 2
                ps = self.ps[bank]
                for b in range(2):
                    S.op("pe", lambda E: E.matmul(ps[0:64, :n], lhsT=wkn[:, b, h * 64:(h + 1) * 64], rhs=latn[:, 2 + b, :n],
                                                  start=(b == 0), stop=(b == 1)), reads=["wkn", ("latn", 2 + b)], writes=[("ps", bank)])
                oi = ocnt[0] % 4
                ocnt[0] += 1
                o, ok = ost[oi], ("p1o", oi)
                S.op("act", lambda E: E.activation(out=o[0:64, :n], in_=ps[0:64, :n], func=AF.Identity), reads=[("ps", bank)], writes=[ok])
                S.op("dve", lambda E: E.tensor_copy(out=o[64:96, :n], in_=KRt[64:96, :n]), reads=["KRt"], writes=[ok])
                r0 = send_kB + h * 96
                if ci < 4:
                    key = ("send", r0, ci)
                    g, lr = self.kt_loc(r0)
                    self.sendkeys[g].append(key)
                    store(o, ok, 96, n, self.sendg[g][lr:lr + 96, t0:t0 + n], [key])
                else:
                    store(o, ok, 96, n, self.ktc[r0:r0 + 96, :], [("ktc", r0)])
            for ti in range(n // 128):
                tt = t0 // 128 + ti
                c = vcnt[0]
                vcnt[0] += 1
                bank = 6 + c % 2
                ps = self.ps[bank]
                for b in range(2):
                    S.op("pe", lambda E: E.matmul(ps[:, :], lhsT=latn[:, 2 + b, ti * 128:(ti + 1) * 128], rhs=wvb[:, b, :],
                                                  start=(b == 0), stop=(b == 1)), reads=["wvb", ("latn", 2 + b)], writes=[("ps", bank)])
                v = vst[c % 2]
                vk = ("vst", c % 2)
                S.op("act", lambda E: E.activation(out=v, in_=ps, func=AF.Identity), reads=[("ps", bank)], writes=[vk])
                if tt < 16:
                    for hb in range(2):
                        g = f"VB{hb}"
                        key = ("sendv", g, tt)
                        self.sendkeys[g].append(key)
                        S.dma("sp", lambda E: E.dma_start(out=self.sendg[g][tt * 128:(tt + 1) * 128, :], in_=v[:, hb * 256:(hb + 1) * 256]),
                              reads=[vk], writes=[key])
                else:
                    S.dma("sp", lambda E: E.dma_start(out=self.vc[(tt - 16) * 128:(tt - 15) * 128, 512:1024], in_=v), reads=[vk],
                          writes=[("vc", "B", tt)])

        if self.stop == "p1b":
            return
        for g in ("KA0", "KA1", "VA0", "VA1"):
            gather_later(g)
        if self.stop == "p1g":
            return

        qchunks = range(4) if last else range(5)

        def qdst(row0, ci, nrows=128):
            t0, n = CH[ci]
            return [(self.qs[row0:row0 + nrows, t0:t0 + n], [("qs", row0, ci)])]
        for h in range(4):
            w = load_w(f"AQ{h}")
            for ci in qchunks:
                rope_block(w, ci, qdst(h * 128, ci))
        for m in range(4):
            w = load_w(f"CQ{m}")
            for ci in qchunks:
                rope_block(w, ci, qdst(512 + m * 128, ci), norm=(self.ppc(l, "cqg"), self.ppc(l, "cqgs")))
        for m in range(4):
            w = load_w(f"DQ{m}")
            for ci in qchunks:
                rope_block(w, ci, qdst(1024 + m * 128, ci))
        wBQ, wBQk = load_w("BQ0")
        for ci in qchunks:
            t0, n = CH[ci]
            load_rope32(ci)
            latent_norm(wBQ, wBQk, ci, 0, "qn")
            for h in range(8):
                ba, bb = 2, 3
                for (bank, wq, wqk) in ((ba, wuq, "wuq"), (bb, wuqs, "wuqs")):
                    for b in range(2):
                        S.op("pe", lambda E: E.matmul(self.ps[bank][0:96, :n], lhsT=wq[:, b, h * 96:(h + 1) * 96], rhs=latn[:, b, :n],
                                                      start=(b == 0), stop=(b == 1)), reads=[wqk, ("latn", b)], writes=[("ps", bank)])
                S.op("dve", lambda E: E.tensor_tensor(out=T[0][0:96, :n], in0=self.ps[ba][0:96, :n], in1=rB[0][0:96, :n], op=ALU.mult),
                     reads=[("ps", ba), ("rB", 0)], writes=[("p1t", 0)])
                S.op("dve", lambda E: E.tensor_tensor(out=T[1][0:96, :n], in0=self.ps[bb][0:96, :n], in1=rB[1][0:96, :n], op=ALU.mult),
                     reads=[("ps", bb), ("rB", 1)], writes=[("p1t", 1)])
                oi = ocnt[0] % 4
                ocnt[0] += 1
                o, ok = ost[oi], ("p1o", oi)
                S.op("dve", lambda E: E.tensor_tensor(out=o[0:96, :n], in0=T[0][0:96, :n], in1=T[1][0:96, :n], op=ALU.add),
                     reads=[("p1t", 0), ("p1t", 1)], writes=[ok])
                r0 = 1536 + h * 96
                store(o, ok, 96, n, self.qs[r0:r0 + 96, t0:t0 + n], [("qs", r0, ci)])
        for gp in range(16):
            w, wk = load_w(f"G{gp * 2}")
            for ci in qchunks:
                t0, n = CH[ci]
                for b in range(2):
                    c = cnt[0]
                    cnt[0] += 1
                    bank = c % 4
                    mm8(bank, w, wk, b * 128, 128, ci)
                    oi = ocnt[0] % 4
                    ocnt[0] += 1
                    o, ok = ost[oi], ("p1o", oi)
                    S.op("act", lambda E: E.activation(out=o[:, :n], in_=self.ps[bank][:, :n], func=AF.Sigmoid), reads=[("ps", bank)], writes=[ok])
                    gi = gp * 2 + b
                    store(o, ok, 128, n, self.gs[gi * 128:(gi + 1) * 128, t0:t0 + n], [("gs", gi, ci)])
        gq_tick(force=True)

    def p2(self, l, last):
        S = self.S
        KT = [self.A("KT", [128, NKB * 128], BF16, i * 16896) for i in range(2)]
        V = [self.A("V", [128, NKB, 128], BF16, 33792 + i * 16896) for i in range(2)]
        QT = self.A("QT", [128, 4, TOK], BF16, 67584)
        NPT = 4
        PT = [self.A("PT", [128, 2, 512], BF16, 86016 + i * 2048) for i in range(NPT)]
        T = [self.A("p2t", [128, 512], F32, 94208 + i * 2048) for i in range(8)]
        accS2 = [self.A("accS", [128, 2, 512], F32, 110592 + i * 4096) for i in range(2)]
        masks = self.A("masks", [128, 10, 512], BF16, 110592)
        ost = [self.A("p2o", [128, 512], BF16, 120832 + i * 1024) for i in range(2)]
        pend = []

        def defer(delay, fn):
            pend.append([delay, fn])

        def tick():
            while pend and pend[0][0] <= 0:
                pend.pop(0)[1]()
            for p in pend:
                p[0] -= 1

        def flush():
            while pend:
                pend.pop(0)[1]()
        qch = list(range(4)) if last else list(range(5))
        for g in self.late:
            self.gather(g)
        st = {"step": 0, "acc": 0, "o": 0, "u": 0}
        lam_init = 0.8 - 0.6 * math.exp(-0.3 * l)

        def load_kt(i, row0, nrows, pbase):
            S.dma("sp", lambda E: E.dma_start(out=KT[i][pbase:pbase + nrows, 0:NC], in_=self.ktc[row0:row0 + nrows, :]),
                  reads=self._ktc_keys(row0, nrows), writes=[("KT", i)])
            g, lr = self.kt_loc(row0)
            gr = self.cg[g][0]
            for r in range(4):
                S.dma("sp", lambda E: E.dma_start(out=KT[i][pbase:pbase + nrows, NC + r * NT:NC + (r + 1) * NT],
                                                  in_=self.recvg[g][r * gr + lr:r * gr + lr + nrows, :]), reads=[("recv", g)], writes=[("KT", i)])

        def load_v(i, col0, vw):
            S.dma("sp", lambda E: E.dma_start(out=V[i][:, 0:2, 0:vw], in_=self.vc[:, col0:col0 + vw].rearrange("(b p) c -> p b c", p=128)),
                  reads=self._vc_keys(), writes=[("V", i)])
            g, lc = self.v_loc(col0)
            for r in range(4):
                S.dma("sp", lambda E: E.dma_start(out=V[i][:, 2 + r * 16:2 + (r + 1) * 16, 0:vw],
                                                  in_=self.recvg[g][r * NT:(r + 1) * NT, lc:lc + vw].rearrange("(b p) c -> p b c", p=128)),
                      reads=[("recv", g)], writes=[("V", i)])

        def set_ones(i):
            S.op("pool", lambda E: E.memset(V[i][:, :, 64:128], 1.0), writes=[("V", i)])

        def pipeline(steps, N, scale, nsg=2, accS=None):
            groups = [steps[i:i + 2] for i in range(0, len(steps), 2)]
            ng = len(groups)
            base = st["step"]
            L = nsg - 1
            for gi in range(ng + L):
                tick()
                if gi < ng:
                    grp = groups[gi]
                    g = base + gi
                    b0 = (g % nsg) * 2
                    for si, sp in enumerate(grp):
                        bank = b0 + si
                        ps = self.ps[bank]
                        has_mask = sp.get("mask") is not None
                        S.op("pe", lambda E: E.matmul(ps[:, :N], lhsT=sp["lhsT"], rhs=sp["rhs"], start=True, stop=not has_mask),
                             reads=[sp["lk"], sp["rk"]], writes=[("ps", bank)])
                        if has_mask:
                            S.op("pe", lambda E: E.matmul(ps[:, :N], lhsT=self.ident_b, rhs=sp["mask"], start=False, stop=True),
                                 reads=["masks", ("c", "ib")], writes=[("ps", bank)])
                    ns = len(grp)
                    pt = PT[g % NPT]
                    src = self.psall[:, b0 * 512:(b0 + ns) * 512].rearrange("p (a n) -> p a n", a=ns)[:, :, :N]
                    S.op("act", lambda E: E.activation(out=pt[:, 0:ns, :N], in_=src, func=AF.Exp, scale=scale),
                         reads=[("ps", b0 + si) for si in range(ns)], writes=[("PT", g % NPT)])
                if gi >= L:
                    grp = groups[gi - L]
                    g = base + gi - L
                    pt = PT[g % NPT]
                    for si, sp in enumerate(grp):
                        for (bank, lhsT, key, start, stop) in sp["pv"]:
                            S.op("pe", lambda E: E.matmul(self.ps[bank][:, :N], lhsT=lhsT, rhs=pt[:, si, :N], start=start, stop=stop),
                                 reads=[key, ("PT", g % NPT)], writes=[("ps", bank)])
                    if accS is not None:
                        ac, ackey = accS
                        nd = (N * 11 // 16) // 2 * 2
                        for eng, c0, c1, kk in (("dve", 0, nd, (ackey, "d")), ("pool", nd, N, (ackey, "p"))):
                            if gi - L == 0:
                                S.op(eng, lambda E: E.tensor_copy(out=ac[:, 1, c0:c1], in_=pt[:, 1, c0:c1]), reads=[("PT", g % NPT)], writes=[kk])
                            else:
                                S.op(eng, lambda E: E.tensor_tensor(out=ac[:, 1, c0:c1], in0=ac[:, 1, c0:c1], in1=pt[:, 1, c0:c1], op=ALU.add),
                                     reads=[("PT", g % NPT), kk], writes=[kk])
            st["step"] += ng

        def epi_bcd(acc, N, dsts, sink_j=None, delays=(1, 5)):
            flush()
            u = st["u"] % 2
            st["u"] += 1
            oc, rc = T[u * 2], T[u * 2 + 1]
            kc_, kr_ = ("p2t", u * 2), ("p2t", u * 2 + 1)
            oi = st["o"] % 2
            st["o"] += 1
            o, ok = ost[oi], ("p2o", oi)
            S.op("act", lambda E: E.activation(out=oc[:, :N], in_=self.ps[acc][:, :N], func=AF.Identity), reads=[("ps", acc)], writes=[kc_])

            def s1():
                S.dma("sp", lambda E: E.dma_start(out=rc[0:64, :N], in_=oc[64:128, :N]), reads=[kc_], writes=[kr_])

            def s2():
                if sink_j is not None:
                    S.op("dve", lambda E: E.tensor_tensor(out=rc[0:64, :N], in0=rc[0:64, :N], in1=self.sinkrow[0:64, sink_j, :N], op=ALU.add),
                         reads=[kr_, "sinkrow"], writes=[kr_])
                S.op("dve", lambda E: E.reciprocal(out=rc[0:64, :N], in_=rc[0:64, :N]), reads=[kr_], writes=[kr_])
                S.op("dve", lambda E: E.tensor_tensor(out=o[0:64, :N], in0=oc[0:64, :N], in1=rc[0:64, :N], op=ALU.mult),
                     reads=[kc_, kr_], writes=[ok])
                for (c0, nc_, dst, keys) in dsts:
                    S.dma("sp", lambda E: E.dma_start(out=dst, in_=o[0:64, c0:c0 + nc_]), reads=[ok], writes=keys)
            defer(delays[0], s1)
            defer(delays[1], s2)

        def epi_a(N, dst, keys, accS):
            flush()
            ac, ackey = accS
            u = st["u"] % 2
            st["u"] += 1
            c0_, c1_, t2, t3 = T[u * 4], T[u * 4 + 1], T[u * 4 + 2], T[u * 4 + 3]
            k0, k1, k2, k3 = [("p2t", u * 4 + i) for i in range(4)]
            oi = st["o"] % 2
            st["o"] += 1
            o, ok = ost[oi], ("p2o", oi)
            S.op("dve", lambda E: E.tensor_copy(out=c0_[:, :N], in_=self.ps[6][:, :N]), reads=[("ps", 6)], writes=[k0])
            S.op("dve", lambda E: E.tensor_copy(out=c1_[:, :N], in_=self.ps[7][:, :N]), reads=[("ps", 7)], writes=[k1])

            S.op("dve", lambda E: E.tensor_copy(out=t2[:, :N], in_=self.ps[4][:, :N]), reads=[("ps", 4)], writes=[k2])

            def s1():
                S.op("pe", lambda E: E.matmul(self.ps[5][:, :N], lhsT=self.ones_f, rhs=ac[:, 1, :N], start=True, stop=True),
                     reads=[(ackey, "d"), (ackey, "p"), ("c", 1)], writes=[("ps", 5)])

            def s2a():
                S.op("dve", lambda E: E.reciprocal(out=t2[:, :N], in_=t2[:, :N]), reads=[k2], writes=[k2])

            def s2b():
                S.op("dve", lambda E: E.reciprocal(out=t3[:, :N], in_=self.ps[5][:, :N]), reads=[("ps", 5)], writes=[k3])

            def s2c():
                S.op("pool", lambda E: E.tensor_tensor(out=c0_[:, :N], in0=c0_[:, :N], in1=t2[:, :N], op=ALU.mult),
                     reads=[k0, k2], writes=[k0])
                S.op("pool", lambda E: E.tensor_tensor(out=c1_[:, :N], in0=c1_[:, :N], in1=t3[:, :N], op=ALU.mult),
                     reads=[k1, k3], writes=[k1])
                S.op("dve", lambda E: E.scalar_tensor_tensor(out=c0_[:, :N], in0=c1_[:, :N], scalar=self.lamt[:, l, 3:4], in1=c0_[:, :N],
                                                             op0=ALU.mult, op1=ALU.add), reads=[k0, k1, "lamt"], writes=[k0])

            def s2d():
                S.op("act", lambda E: E.activation(out=t2[:, :N], in_=c0_[:, :N], func=AF.Square), reads=[k0], writes=[k2])

            def s3():
                S.op("pe", lambda E: E.matmul(self.ps[5][:, :N], lhsT=self.ones_f, rhs=t2[:, :N], start=True, stop=True),
                     reads=[k2, ("c", 1)], writes=[("ps", 5)])

            def s4():
                S.op("act", lambda E: E.activation(out=t3[:, :N], in_=self.ps[5][:, :N], func=AF.Ln, scale=1.0 / 128,
                                                   bias=self.ppt[:, PP_EPS:PP_EPS + 1]), reads=[("ps", 5), "pp"], writes=[k3])
                S.op("act", lambda E: E.activation(out=t3[:, :N], in_=t3[:, :N], func=AF.Exp, scale=-0.5), reads=[k3], writes=[k3])

            def s5():
                S.op("dve", lambda E: E.scalar_tensor_tensor(out=c1_[:, :N], in0=c0_[:, :N], scalar=self.ppc(l, "subl"), in1=t3[:, :N],
                                                             op0=ALU.mult, op1=ALU.mult), reads=[k0, k3, "pp"], writes=[k1])
                S.op("dve", lambda E: E.tensor_scalar(out=o[:, :N], in0=c1_[:, :N], scalar1=1.0 - lam_init, scalar2=None, op0=ALU.mult),
                     reads=[k1], writes=[ok])
                S.dma("sp", lambda E: E.dma_start(out=dst, in_=o[:, :N]), reads=[ok], writes=keys)
            defer(2, s1)
            defer(4, s2a)
            defer(8, s2b)
            defer(11, s2c)
            defer(13, s2d)
            defer(15, s3)
            defer(18, s4)
            defer(21, s5)

        def qkeys(row0, nrows=128):
            return [("qs", row0, ci) for ci in qch]

        S.dma("sp", lambda E: E.dma_start(out=QT, in_=self.qs[0:512, :].rearrange("(b p) t -> p b t", p=128)),
              reads=[k for h in range(4) for k in qkeys(h * 128)], writes=["QT"])
        S.op("pool", lambda E: E.memset(KT[0][64:128, :], 0.0), writes=[("KT", 0)])
        S.op("pool", lambda E: E.memset(KT[1][0:64, :], 0.0), writes=[("KT", 1)])
        for h in range(4):
            i = h % 2
            load_kt(0, h * 128, 64, 0)
            load_kt(1, h * 128 + 64, 64, 64)
            load_v(i, h * 128, 128)
            for ci in qch:
                t0, N = CH[ci]
                nkb = NKB if ci < 4 else 2
                steps = []
                for kb in range(nkb):
                    for m in range(2):
                        steps.append(dict(lhsT=KT[m][:, kb * 128:(kb + 1) * 128], lk=("KT", m),
                                          rhs=QT[:, h, t0:t0 + N], rk="QT",
                                          pv=[(6 + m, V[i][:, kb, :], ("V", i), kb == 0, kb == nkb - 1)] +
                                             ([(4, self.ones_b, ("c", "ob"), kb == 0, kb == nkb - 1)] if m == 0 else [])))
                au = st["au"] = st.get("au", 0) + 1
                accS = (accS2[au % 2], ("accS", au % 2))
                pipeline(steps, N, 0.125, nsg=2, accS=accS)
                epi_a(N, self.os_[h * 128:(h + 1) * 128, t0:t0 + N], [("os", h, ci)], accS)
        flush()
        set_ones(0)
        set_ones(1)
        for h in range(8):
            i = h % 2
            if h % 4 == 0:
                S.dma("sp", lambda E: E.dma_start(out=QT[0:96, :, :], in_=self.qs[1536 + h * 96:1536 + (h + 4) * 96, :].rearrange("(b p) t -> p b t", p=96)),
                      reads=[k for hh in range(h, h + 4) for k in qkeys(1536 + hh * 96)], writes=["QT"])
            load_kt(i, 512 + h * 96, 96, 0)
            load_v(i, 512 + h * 64, 64)
            for ci in qch:
                t0, N = CH[ci]
                nkb = NKB if ci < 4 else 2
                acc = 6 + st["acc"] % 2
                st["acc"] += 1
                steps = [dict(lhsT=KT[i][0:96, kb * 128:(kb + 1) * 128], lk=("KT", i), rhs=QT[0:96, h % 4, t0:t0 + N], rk="QT",
                              pv=[(acc, V[i][:, kb, :], ("V", i), kb == 0, kb == nkb - 1)]) for kb in range(nkb)]
                pipeline(steps, N, 96 ** -0.5, nsg=3)
                r0 = 512 + h * 64
                epi_bcd(acc, N, [(0, N, self.os_[r0:r0 + 64, t0:t0 + N], [("os", r0 // 128, ci, h % 2)])])
        S.dma("sp", lambda E: E.dma_start(out=QT, in_=self.qs[512:1024, :].rearrange("(b p) t -> p b t", p=128)),
              reads=[k for m in range(4) for k in qkeys(512 + m * 128)], writes=["QT"])
        for j in range(2):
            i = j
            S.op("pool", lambda E: E.memset(KT[i][(1 - j) * 64:(2 - j) * 64, :], 0.0), writes=[("KT", i)])
            load_kt(i, 1280 + j * 64, 64, j * 64)
            load_v(i, 1024 + j * 64, 64)
            for m in range(4):
                for ci in qch:
                    t0, N = CH[ci]
                    nkb = NKB if ci < 4 else 2
                    acc = 6 + st["acc"] % 2
                    st["acc"] += 1
                    steps = [dict(lhsT=KT[i][:, kb * 128:(kb + 1) * 128], lk=("KT", i),
                                  rhs=QT[:, m, t0:t0 + N], rk="QT",
                                  pv=[(acc, V[i][:, kb, :], ("V", i), kb == 0, kb == nkb - 1)]) for kb in range(nkb)]
                    pipeline(steps, N, 0.125, nsg=3)
                    hh = 4 * j + m
                    r0 = 1024 + hh * 64
                    epi_bcd(acc, N, [(0, N, self.os_[r0:r0 + 64, t0:t0 + N], [("os", r0 // 128, ci, hh % 2)])])
        S.dma("sp", lambda E: E.dma_start(out=QT, in_=self.qs[1024:1536, :].rearrange("(b p) t -> p b t", p=128)),
              reads=[k for m in range(4) for k in qkeys(1024 + m * 128)], writes=["QT"])
        flush()
        S.barrier()
        S.dma("pool", lambda E: E.dma_start(out=masks, in_=self.dmask.rearrange("m p c -> p m c")), writes=["masks"])
        KD, KC = KT[0], KT[1]
        DW = NC + NT
        for j in range(2):
            oth = slice((1 - j) * 64, (2 - j) * 64)
            own = slice(j * 64, (j + 1) * 64)
            S.op("pool", lambda E: E.memset(KD[oth, j * DW:(j + 1) * DW], 0.0), writes=[("KT", 0)])
            S.op("pool", lambda E: E.memset(KC[oth, j * 1024:(j + 1) * 1024], 0.0), writes=[("KT", 1)])
            S.dma("sp", lambda E: E.dma_start(out=KD[own, j * DW:j * DW + NC], in_=self.ktc[1408 + j * 64:1472 + j * 64, :]),
                  reads=[("ktc", 1408)], writes=[("KT", 0)])
            S.dma("sp", lambda E: E.dma_start(out=KD[own, j * DW + NC:(j + 1) * DW], in_=self.kdl[own, :]),
                  reads=[("kdl", ci) for ci in range(4)], writes=[("KT", 0)])
            for r in range(4):
                for e in range(2):
                    c0 = j * 1024 + (r * 2 + e) * 128
                    S.dma("sp", lambda E: E.dma_start(out=KC[own, c0:c0 + 128],
                                                      in_=self.recvg["KCD"][r * 256 + 128 + j * 64:r * 256 + 192 + j * 64, e * 1920:e * 1920 + 128]),
                          reads=[("recv", "KCD")], writes=[("KT", 1)])
        for j in range(2):
            S.dma("sp", lambda E: E.dma_start(out=V[j][:, 0:2, 0:64], in_=self.vc[:, 1152 + j * 64:1216 + j * 64].rearrange("(b p) c -> p b c", p=128)),
                  reads=self._vc_keys(), writes=[("V", j)])
            S.dma("sp", lambda E: E.dma_start(out=V[j][:, 2:18, 0:64], in_=self.vdl[:, j * 64:(j + 1) * 64].rearrange("(b p) c -> p b c", p=128)),
                  reads=[("vdl", tt) for tt in range(16)], writes=[("V", j)])
            for r in range(4):
                for e in range(2):
                    S.dma("sp", lambda E: E.dma_start(out=V[j][:, 18 + r * 2 + e, 0:64],
                                                      in_=self.recvg["VCD"][r * NT + e * 1920:r * NT + e * 1920 + 128, 128 + j * 64:192 + j * 64]),
                          reads=[("recv", "VCD")], writes=[("V", j)])
        nq = 16 if last else 18
        for j in range(2):
            for n in range(nq):
                blocks = [(KD[:, j * DW:j * DW + 128], ("KT", 0), 0, None), (KD[:, j * DW + 128:j * DW + 256], ("KT", 0), 1, None)]
                if n < 16:
                    def loc(b, mi):
                        return (KD[:, j * DW + NC + b * 128:j * DW + NC + (b + 1) * 128], ("KT", 0), 2 + b, mi)
                    def cand(r, e, mi):
                        c0 = j * 1024 + (r * 2 + e) * 128
                        return (KC[:, c0:c0 + 128], ("KT", 1), 18 + r * 2 + e, mi)
                    if n > 0:
                        blocks.append(loc(n - 1, 0))
                    else:
                        blocks += [cand(r, 1, 2 + r) for r in range(4)]
                    blocks.append(loc(n, None))
                    if n < 15:
                        blocks.append(loc(n + 1, 1))
                    else:
                        blocks += [cand(r, 0, 6 + r) for r in range(4)]
                acc = 6 + st["acc"] % 2
                st["acc"] += 1
                nb = len(blocks)
                steps = [dict(lhsT=kt, lk=kk, rhs=QT[:, :, n * 128:(n + 1) * 128], rk="QT",
                              mask=(None if mi is None else masks[:, mi, :]),
                              pv=[(acc, V[j][:, vb, :], ("V", j), bi == 0, bi == nb - 1)])
                         for bi, (kt, kk, vb, mi) in enumerate(blocks)]
                pipeline(steps, 512, 0.125, nsg=3)
                ci = min(n // 4, 4)
                dsts = []
                for g in range(4):
                    hh = 4 * j + g
                    r0 = 1536 + hh * 64
                    dsts.append((g * 128, 128, self.os_[r0:r0 + 64, n * 128:(n + 1) * 128], [("os", r0 // 128, ci, hh % 2, n)]))
                epi_bcd(acc, 512, dsts, sink_j=j, delays=(1, 3))
        flush()

    @staticmethod
    def kt_loc(row0):
        if row0 < 512:
            h = row0 // 128
            return f"KA{h // 2}", (h % 2) * 128 + row0 % 128
        if row0 < 1280:
            h = (row0 - 512) // 96
            return f"KB{h // 2}", (h % 2) * 96
        return "KCD", row0 - 1280

    @staticmethod
    def v_loc(col0):
        if col0 < 512:
            return f"VA{col0 // 256}", col0 % 256
        if col0 < 1024:
            return f"VB{(col0 - 512) // 256}", (col0 - 512) % 256
        return "VCD", col0 - 1024

    def gather(self, g):
        rows, cols = self.cg[g]
        self.S.collective(lambda E: E.collective_compute("AllGather", ALU.bypass, replica_groups=[[0, 1, 2, 3], [4, 5, 6, 7]],
                                                         ins=[self.sendg[g].opt()], outs=[self.recvg[g].opt()]),
                          reads=self.sendkeys[g], writes=[("recv", g)])

    def _ktc_keys(self, row0, nrows):
        if row0 < 512 or row0 >= 1280:
            return [("ktc", (row0 // 128) * 128)]
        return [("ktc", row0)]

    def _vc_keys(self):
        return [("vc", "AV", 16), ("vc", "AV", 17), ("vc", "CDV", 16), ("vc", "CDV", 17), ("vc", "B", 16), ("vc", "B", 17)]

    def os_keys(self, ci):
        ks = [("os", h, ci) for h in range(4)]
        ks += [("os", c, ci, e) for c in range(4, 12) for e in range(2)]
        nlist = range(ci * 4, ci * 4 + 4) if ci < 4 else (16, 17)
        ks += [("os", c, ci, e, n) for c in range(12, 16) for e in range(2) for n in nlist]
        return ks

    def p3(self, l, last):
        S = self.S
        Wb = self.A("Wb", [128, 16, D], BF16, 0)
        wo = self.A("wo", [128, 8, D], BF16, 32768)
        oT = [self.A("oT", [128, 16, 512], BF16, 49152 + i * 16384) for i in range(2)]
        gt = [self.A("gt", [128, 4, 512], BF16, 81920 + i * 4096) for i in range(2)]
        tk = [self.A("tk", [128, 512], F32, 90112 + i * 2048) for i in range(4)]
        sT = self.A("sT", [128, 8, 512], BF16, 98304)
        for k in range(4):
            S.dma("pool", lambda E: E.dma_start(out=Wb[:, k * 4:(k + 1) * 4, :], in_=self.wbr[l, k].rearrange("(c p) o -> p c o", p=128)),
                  writes=[("Wb", k)])
        S.dma("pool", lambda E: E.dma_start(out=wo, in_=self.wout[l].rearrange("(c p) o -> p c o", p=128)), writes=["wo"])
        qch = list(range(4)) if last else list(range(5))
        gc = 0
        for n_i, ci in enumerate(qch):
            t0, N = CH[ci]
            jm = 1 if ci == 4 else 0
            o = oT[n_i % 2]
            S.dma("sp", lambda E: E.dma_start(out=o[:, :, :N], in_=self.os_[:, t0:t0 + N].rearrange("(c p) t -> p c t", p=128)),
                  reads=self.os_keys(ci), writes=[("oT", n_i % 2)])
            for j in range(8):
                g = gt[gc % 2]
                gk = ("gt", gc % 2)
                gc += 1
                S.dma("sp", lambda E: E.dma_start(out=g[:, :, :N],
                                                  in_=self.gs.rearrange("(k j p) t -> p k j t", k=4, j=8)[:, :, j, t0:t0 + N]),
                      reads=[("gs", k * 8 + j, ci) for k in range(4)], writes=[gk])
                for k in range(4):
                    bank = k
                    for c in range(4):
                        S.op("pe", lambda E: E.matmul(self.ps[bank][:, :N], lhsT=Wb[:, k * 4 + c, j * 128:(j + 1) * 128], rhs=o[:, k * 4 + c, :N],
                                                      start=(c == 0), stop=(c == 3)), reads=[("Wb", k), ("oT", n_i % 2)], writes=[("ps", bank)])
                    S.op("dve", lambda E: E.tensor_tensor(out=tk[k][:, :N], in0=self.ps[bank][:, :N], in1=g[:, k, :N], op=ALU.mult),
                         reads=[("ps", bank), gk], writes=[("tk", k)])
                S.op("pool", lambda E: E.tensor_tensor(out=tk[0][:, :N], in0=tk[0][:, :N], in1=tk[1][:, :N], op=ALU.add),
                     reads=[("tk", 0), ("tk", 1)], writes=[("tk", 0)])
                S.op("pool", lambda E: E.tensor_tensor(out=tk[2][:, :N], in0=tk[2][:, :N], in1=tk[3][:, :N], op=ALU.add),
                     reads=[("tk", 2), ("tk", 3)], writes=[("tk", 2)])
                S.op("dve", lambda E: E.tensor_tensor(out=sT[:, j, :N], in0=tk[0][:, :N], in1=tk[2][:, :N], op=ALU.add),
                     reads=[("tk", 0), ("tk", 2)], writes=[("sT", j)])
            for i in range(8):
                bank = 4 + i % 4
                for j in range(8):
                    S.op("pe", lambda E: E.matmul(self.ps[bank][:, :N], lhsT=wo[:, j, i * 128:(i + 1) * 128], rhs=sT[:, j, :N],
                                                  start=(j == 0), stop=(j == 7)), reads=["wo", ("sT", j)], writes=[("ps", bank)])
                S.op("dve", lambda E: E.scalar_tensor_tensor(out=self.xT[:, i, t0:t0 + N], in0=self.ps[bank][:, :N], scalar=self.mod(l, 2, i, jm),
                                                             in1=self.xT[:, i, t0:t0 + N], op0=ALU.mult, op1=ALU.add),
                     reads=[("ps", bank), ("xT", ci), "modT"], writes=[("xT", ci)])
        self.dump(f"xm{l}", self.xT, [("xT", c) for c in range(5)])

    def p4(self, l, last):
        S = self.S
        H2 = self.A("H2T", [128, 8, 2308], BF16, 0)
        WU = [self.A("WU", [128, 8, 1024], BF16, 36992 + i * 16384) for i in range(2)]
        WD = [self.A("WD", [128, 4, D], BF16, 69760 + i * 8192) for i in range(2)]
        T = [self.A("p4t", [128, 512], F32, 86144 + i * 2048) for i in range(6)]
        aT = [self.A("aT", [128, 4, 512], BF16, 98432 + i * 4096) for i in range(2)]
        cand = self.A("cand", [128, 4, 2, 8], BF16, 114816)
        flg = self.A("flg", [128, 2, 4, 8], F32, 114944)
        ctmp = self.A("ctmp", [128, 4, 8], F32, 115200)
        ctmp2 = self.A("ctmp2", [128, 8], F32, 115328)
        qch = list(range(4)) if last else list(range(5))
        dcol = lambda ci: 1 + CH[ci][0] if ci < 4 else 2051
        self.modulate(l, 3, 4, H2, dcol, "H2", qch, 104576, 108672, 110720)
        S.barrier()
        stg = self.A("stg", [128, 2, 8], BF16, 115392)
        S.op("dve", lambda E: E.tensor_copy(out=stg[:, 0, :], in_=H2[:, :, 1]), reads=[("H2", 0)], writes=["stg"])
        S.op("dve", lambda E: E.tensor_copy(out=stg[:, 1, :], in_=H2[:, :, 2048]), reads=[("H2", 3)], writes=["stg"])
        S.dma("sp", lambda E: E.dma_start(out=self.send2, in_=stg.rearrange("p e k -> p (e k)")), reads=["stg"], writes=["send2"])
        S.collective(lambda E: E.collective_compute("AllGather", ALU.bypass, replica_groups=[[0, 1, 2, 3], [4, 5, 6, 7]],
                                                    ins=[self.send2.opt()], outs=[self.recv2.opt()]),
                     reads=["send2"], writes=["recv2"])
        S.dma("sp", lambda E: E.dma_start(out=cand, in_=self.recv2.rearrange("(r p) (e k) -> p r e k", p=128, e=2)), reads=["recv2"], writes=["cand"])
        S.dma("sp", lambda E: E.dma_start(out=flg, in_=self.hflag.rearrange("a p r k -> p a r k")), writes=["flg"])
        for a, e, col in ((0, 1, 0), (1, 0, 2049)):
            S.op("dve", lambda E: E.tensor_tensor(out=ctmp, in0=cand[:, :, e, :], in1=flg[:, a], op=ALU.mult),
                 reads=["cand", "flg"], writes=["ctmp"])
            S.op("dve", lambda E: E.tensor_reduce(out=ctmp2, in_=ctmp.rearrange("p r k -> p k r"), axis=mybir.AxisListType.X, op=ALU.add),
                 reads=["ctmp"], writes=["ctmp2"])
            S.op("dve", lambda E: E.tensor_copy(out=H2[:, :, col], in_=ctmp2), reads=["ctmp2"], writes=[("H2h", col)])
        S.op("pool", lambda E: E.memset(H2[:, :, 2050:2051], 0.0), writes=[("H2h", 2050)])
        S.op("pool", lambda E: E.memset(H2[:, :, 2307:2308], 0.0), writes=[("H2h", 2307)])
        tch = []
        for s0 in (510, 1020, 1530, 0, 2040):
            n = min(510, NT - s0)
            xk = sorted(set([s0 // 512, (s0 + n - 1) // 512]))
            hk = [("H2", c) for c in sorted(set([max(s0 - 1, 0) // 512, min(s0 + n, NT - 1) // 512]))]
            if s0 == 0:
                hk.append(("H2h", 0))
            if s0 + n == NT:
                hk.append(("H2h", 2049))
            tch.append((s0, n, s0, 0, xk, hk))
        if not last:
            tch.append((2050, 256, 2048, 1, [4], [("H2", 4), ("H2h", 2050), ("H2h", 2307)]))
        groups = [(0, 4), (4, 4), (8, 4), (12, 4), (16, 4), (20, 2)]
        cb = PL0 + l * PLN
        ac = 0
        def load_ffn_w(gi):
            c0, ncg = groups[gi]
            wu, wd = WU[gi % 2], WD[gi % 2]
            S.dma("pool", lambda E: E.dma_start(out=wu[:, :, 0:ncg * 128],
                                                in_=self.wup[l, :, c0 * 128:(c0 + ncg) * 128].rearrange("(kc p) c -> p kc c", p=128)),
                  writes=[("WU", gi % 2)])
            S.dma("pool", lambda E: E.dma_start(out=wu[:, :, 512:512 + ncg * 128],
                                                in_=self.wup[l, :, DFF + c0 * 128:DFF + (c0 + ncg) * 128].rearrange("(kc p) c -> p kc c", p=128)),
                  writes=[("WU", gi % 2)])
            S.dma("pool", lambda E: E.dma_start(out=wd[:, 0:ncg, :], in_=self.wdn[l, c0 * 128:(c0 + ncg) * 128, :].rearrange("(c p) o -> p c o", p=128)),
                  writes=[("WD", gi % 2)])
        load_ffn_w(0)
        for gi, (c0, ncg) in enumerate(groups):
            wu, wd = WU[gi % 2], WD[gi % 2]
            if gi + 1 < len(groups):
                load_ffn_w(gi + 1)
            for (a0, n, x0, jm, xkeys, HKALL) in tch:
                at = aT[ac % 2]
                ak = ("aT", ac % 2)
                ac += 1
                for cc in range(ncg):
                    ch = c0 + cc
                    pv, pg = self.ps[(cc % 2) * 2], self.ps[(cc % 2) * 2 + 1]
                    for (ps_, bank, wc0) in ((pv, (cc % 2) * 2, cc * 128), (pg, (cc % 2) * 2 + 1, 512 + cc * 128)):
                        for kc in range(8):
                            S.op("pe", lambda E: E.matmul(ps_[:, :n + 2], lhsT=wu[:, kc, wc0:wc0 + 128], rhs=H2[:, kc, a0:a0 + n + 2],
                                                          start=(kc == 0), stop=(kc == 7)), reads=[("WU", gi % 2)] + HKALL, writes=[("ps", bank)])
                    tb = (cc % 2) * 3
                    for (ps_, bank, cidx, t, tkey) in ((pv, (cc % 2) * 2, ch, T[tb], ("p4t", tb)), (pg, (cc % 2) * 2 + 1, 22 + ch, T[tb + 1], ("p4t", tb + 1))):
                        w0 = self.ppt[:, cb + PLO["convw"] + cidx * 3 + 0:cb + PLO["convw"] + cidx * 3 + 1]
                        w1 = self.ppt[:, cb + PLO["convw"] + cidx * 3 + 1:cb + PLO["convw"] + cidx * 3 + 2]
                        w2 = self.ppt[:, cb + PLO["convw"] + cidx * 3 + 2:cb + PLO["convw"] + cidx * 3 + 3]
                        bb = self.ppt[:, cb + PLO["convb"] + cidx:cb + PLO["convb"] + cidx + 1]
                        S.op("dve", lambda E: E.tensor_scalar(out=t[:, :n], in0=ps_[:, 1:n + 1], scalar1=w1, scalar2=bb, op0=ALU.mult, op1=ALU.add),
                             reads=[("ps", bank), "pp"], writes=[tkey])
                        S.op("dve", lambda E: E.scalar_tensor_tensor(out=t[:, :n], in0=ps_[:, 0:n], scalar=w0, in1=t[:, :n], op0=ALU.mult, op1=ALU.add),
                             reads=[("ps", bank), "pp", tkey], writes=[tkey])
                        S.op("dve", lambda E: E.scalar_tensor_tensor(out=t[:, :n], in0=ps_[:, 2:n + 2], scalar=w2, in1=t[:, :n], op0=ALU.mult, op1=ALU.add),
                             reads=[("ps", bank), "pp", tkey], writes=[tkey])
                    S.op("act", lambda E: E.activation(out=T[tb + 2][:, :n], in_=T[tb + 1][:, :n], func=AF.Silu), reads=[("p4t", tb + 1)], writes=[("p4t", tb + 2)])
                    S.op("pool", lambda E: E.tensor_tensor(out=at[:, cc, :n], in0=T[tb][:, :n], in1=T[tb + 2][:, :n], op=ALU.mult),
                         reads=[("p4t", tb), ("p4t", tb + 2)], writes=[ak])
                for i in range(8):
                    bank = 4 + i % 4
                    for cc in range(ncg):
                        S.op("pe", lambda E: E.matmul(self.ps[bank][:, :n], lhsT=wd[:, cc, i * 128:(i + 1) * 128], rhs=at[:, cc, :n],
                                                      start=(cc == 0), stop=(cc == ncg - 1)), reads=[("WD", gi % 2), ak], writes=[("ps", bank)])
                    S.op("dve", lambda E: E.scalar_tensor_tensor(out=self.xT[:, i, x0:x0 + n], in0=self.ps[bank][:, :n], scalar=self.mod(l, 5, i, jm),
                                                                 in1=self.xT[:, i, x0:x0 + n], op0=ALU.mult, op1=ALU.add),
                         reads=[("ps", bank), "modT"] + [("xT", c) for c in xkeys], writes=[("xT", c) for c in xkeys])
        self.dump(f"x{l + 1}", self.xT, [("xT", c) for c in range(5)])


PP_EPS = 0
PP_FN = 1
PL0 = 9
PLO = {}
_o = 0
for _n, _w in (("convw", 132), ("convb", 44), ("qn", 2), ("kvn", 2), ("cqg", 1), ("cqgs", 1), ("ckg", 1), ("ckgs", 1),
               ("subl", 1), ("lam", 256), ("sink", 8)):
    PLO[_n] = _o
    _o += _w
PLN = _o
NPP = PL0 + NL * PLN


def _consts():
    c = np.zeros((4, 128, 128), np.float32)
    c[0] = np.eye(128)
    c[1] = 1.0
    c[2, :64, :64] = 1.0
    c[2, 64:, 64:] = 1.0
    for m in range(64):
        c[3, 64 + m, m] = 1.0
    return c


def _fm(v):
    v = np.asarray(v, np.float32)
    return np.ascontiguousarray(v.reshape(-1, 128).T)


_CACHE = {}


def _get_nc(nlayers=NL, dbg=None):
    key = (nlayers, tuple(d[0] for d in (dbg or [])))
    if key not in _CACHE:
        _CACHE[key] = Builder(nlayers, dbg).nc
    return _CACHE[key]


def _host_inputs(inp):
    x = np.asarray(inp["x"], np.float32)
    maps = []
    w_in = np.asarray(inp["w_in"], np.float32)
    wf = np.zeros((NL, D, NWF), np.float32)
    valid = WF_COLS >= 0
    wf[:, :, valid] = w_in[:, :, WF_COLS[valid]]
    shared = {
        "w_mod": np.ascontiguousarray(inp["w_mod"], np.float32),
        "wf": wf,
        "w_branch": np.ascontiguousarray(inp["w_branch"], np.float32),
        "w_out": np.ascontiguousarray(inp["w_out"], np.float32),
        "w_up": np.ascontiguousarray(inp["ffn_w_up"], np.float32),
        "w_down": np.ascontiguousarray(inp["ffn_w_down"], np.float32),
        "consts": _consts(),
    }
    bm = np.asarray(inp["b_mod"], np.float32)
    shared["bmodT"] = np.ascontiguousarray(
        np.repeat(bm.reshape(NL, 48, 128).transpose(0, 2, 1)[:, :, :, None], 2, axis=3))
    pp = np.zeros((128, NPP), np.float32)
    pp[:, PP_EPS] = EPS
    pp[:, PP_FN:PP_FN + 8] = _fm(inp["final_norm"])
    shared["pp"] = pp
    wuq = np.ascontiguousarray(inp["mla_w_uq"], np.float32)
    idx = np.arange(768)
    hh, dd = idx // 96, idx % 96
    sidx = np.where(dd >= 64, hh * 96 + 64 + ((dd - 64) ^ 1), idx)
    shared["wuq"] = wuq
    shared["wuqs"] = np.ascontiguousarray(wuq[:, :, sidx])
    wukv = np.asarray(inp["mla_w_ukv"], np.float32).reshape(NL, 256, 8, 128)
    shared["wkn"] = np.ascontiguousarray(wukv[:, :, :, :64].reshape(NL, 256, 512))
    shared["wvb"] = np.ascontiguousarray(wukv[:, :, :, 64:].reshape(NL, 256, 512))
    p64 = np.arange(128) % 64
    for l in range(NL):
        b0 = PL0 + l * PLN
        cw = np.asarray(inp["ffn_conv_w"][l], np.float32)
        for k in range(3):
            pp[:, b0 + PLO["convw"] + k:b0 + PLO["convw"] + 132:3] = _fm(cw[k])
        pp[:, b0 + PLO["convb"]:b0 + PLO["convb"] + 44] = _fm(inp["ffn_conv_b"][l])
        pp[:, b0 + PLO["qn"]:b0 + PLO["qn"] + 2] = _fm(inp["mla_q_norm"][l])
        pp[:, b0 + PLO["kvn"]:b0 + PLO["kvn"] + 2] = _fm(inp["mla_kv_norm"][l])
        qg = np.asarray(inp["gqa_q_norm"][l], np.float32)
        kg = np.asarray(inp["gqa_k_norm"][l], np.float32)
        pp[:, b0 + PLO["cqg"]] = qg[p64]
        pp[:, b0 + PLO["cqgs"]] = qg[p64 ^ 1]
        pp[:, b0 + PLO["ckg"]] = kg[p64]
        pp[:, b0 + PLO["ckgs"]] = kg[p64 ^ 1]
        pp[:, b0 + PLO["subl"]] = np.asarray(inp["diff_subln"][l], np.float32)
        pp[:, b0 + PLO["lam"]:b0 + PLO["lam"] + 256] = np.asarray(inp["diff_lambda"][l], np.float32).reshape(1, 256)
        pp[:, b0 + PLO["sink"]:b0 + PLO["sink"] + 8] = np.asarray(inp["swa_sink"][l], np.float32).reshape(1, 8)
    kk = np.arange(128)[:, None]
    qq = np.arange(128)[None, :]
    lo = np.where(kk >= qq, 0.0, NEGM).astype(np.float32)
    hi = np.where(kk <= qq, 0.0, NEGM).astype(np.float32)
    lo4, hi4 = np.tile(lo, (1, 4)), np.tile(hi, (1, 4))
    neg4 = np.full((128, 512), NEGM, np.float32)
    for core in range(8):
        b, r = core // 4, core % 4
        m = dict(shared)
        t = (r * NT + np.arange(NT)).astype(np.int32)
        row = (t // 64).astype(np.float32)
        col = (t % 64).astype(np.float32)
        def tables(axis_dim):
            inv = (np.float32(10000.0) ** (-(np.arange(0, axis_dim, 2, dtype=np.float32)) / np.float32(axis_dim))).astype(np.float32)
            ang = np.concatenate([row[:, None] * inv, col[:, None] * inv], axis=-1).astype(np.float32)
            return np.cos(ang).astype(np.float32), np.sin(ang).astype(np.float32)
        c64, s64 = tables(32)
        c32, s32 = tables(16)
        r64 = np.zeros((2, 128, TOK), np.float32)
        r64[0, :, NT:] = 1.0
        d = np.arange(64)
        sign = np.where(d % 2 == 0, -1.0, 1.0).astype(np.float32)
        for rep in range(2):
            r64[0, rep * 64:(rep + 1) * 64, :NT] = c64[:, d // 2].T
            r64[1, rep * 64:(rep + 1) * 64, :NT] = (s64[:, d // 2] * sign[None, :]).T
        r32 = np.zeros((2, 96, TOK), np.float32)
        r32[0] = 1.0
        d2 = np.arange(32)
        sign2 = np.where(d2 % 2 == 0, -1.0, 1.0).astype(np.float32)
        r32[0, 64:, :NT] = c32[:, d2 // 2].T
        r32[1, 64:, :NT] = (s32[:, d2 // 2] * sign2[None, :]).T
        m["rope64"] = r64
        m["rope32"] = r32
        dm = np.zeros((10, 128, 512), np.float32)
        dm[0], dm[1] = lo4, hi4
        hf = np.zeros((2, 128, 4, 8), np.float32)
        for rr in range(4):
            dm[2 + rr] = lo4 if rr == r - 1 else neg4
            dm[6 + rr] = hi4 if rr == r + 1 else neg4
            hf[0, :, rr, :] = 1.0 if rr == r - 1 else 0.0
            hf[1, :, rr, :] = 1.0 if rr == r + 1 else 0.0
        m["dmask"] = dm
        m["hflag"] = hf
        m["x"] = np.ascontiguousarray(x[b, r * NT:(r + 1) * NT])
        m["ctx"] = np.ascontiguousarray(inp["ctx"][b], np.float32)
        cT = np.stack([_fm(inp["c"][b]), _fm(inp["c_ctx"])], axis=-1)
        m["cT"] = np.ascontiguousarray(cT, np.float32)
        maps.append(m)
    return maps


def kernel(**inputs):
    nc = _get_nc()
    maps = _host_inputs(inputs)
    names = set(t for t in _input_names(nc))
    maps = [{k: v for k, v in m.items() if k in names} for m in maps]
    res = run_bass_kernel_spmd(nc, maps, core_ids=list(range(8)))
    out = np.zeros((2, 8192, D), np.float32)
    for core in range(8):
        b, r = core // 4, core % 4
        out[b, r * NT:(r + 1) * NT] = res.results[core]["y"]
    return out


def _input_names(nc):
    return ["x", "ctx", "cT", "w_mod", "bmodT", "wf", "wuq", "wuqs", "wkn", "wvb", "w_branch", "w_out",
            "w_up", "w_down", "pp", "rope64", "rope32", "dmask", "hflag", "consts"]
```

```python
import math
from contextlib import ExitStack
import numpy as np
import ml_dtypes
import concourse.bass as bass
import concourse.mybir as mybir
from concourse.bass_utils import run_bass_kernel_spmd

F32 = mybir.dt.float32
BF16 = mybir.dt.bfloat16
AF = mybir.ActivationFunctionType
ALU = mybir.AluOpType

D = 1024
NL = 2
NT = 2048
NC = 256
TOK = NT + NC
CH = [(0, 512), (512, 512), (1024, 512), (1536, 512), (2048, 256)]
EPS = 1e-6
NKB = 66
KTROWS = 1536
VW = 1280
RANKROWS = KTROWS + (NT * VW) // NT
DFF = 2816
NEGM = -30000.0

def _wf_layout():
    off = {}
    aq, ak, av = 0, 512, 1024
    bq, bkv = 1536, 1792
    cq, ck, cv = 2080, 2592, 2720
    dq, dk, dv = 2848, 3360, 3488
    g0 = 3616
    cols = []
    def add(name, idx):
        off[name] = sum(len(c) for c in cols)
        cols.append(np.asarray(idx, dtype=np.int64))
    def sw(idx):
        idx = np.asarray(idx)
        return np.where(idx >= 0, idx ^ 1, -1)
    r = np.arange
    for h in range(4):
        add(f"AQ{h}", aq + h * 128 + r(128)); add(f"AQ{h}s", sw(aq + h * 128 + r(128)))
    for h in range(4):
        add(f"AK{h}", ak + h * 128 + r(128)); add(f"AK{h}s", sw(ak + h * 128 + r(128)))
    for m in range(4):
        idx = np.concatenate([cq + m * 64 + r(64), cq + (4 + m) * 64 + r(64)])
        add(f"CQ{m}", idx); add(f"CQ{m}s", sw(idx))
    add("CK", ck + r(128)); add("CKs", sw(ck + r(128)))
    for m in range(4):
        idx = np.concatenate([dq + m * 64 + r(64), dq + (4 + m) * 64 + r(64)])
        add(f"DQ{m}", idx); add(f"DQ{m}s", sw(idx))
    add("DK", dk + r(128)); add("DKs", sw(dk + r(128)))
    add("BQ0", bq + r(128)); add("BQ1", bq + 128 + r(128))
    add("BC0", bkv + r(128)); add("BC1", bkv + 128 + r(128))
    kr = np.concatenate([np.full(64, -1), bkv + 256 + r(32), np.full(32, -1)])
    add("KR", kr); add("KRs", sw(kr))
    for k in range(4):
        for j in range(8):
            add(f"G{k*8+j}", g0 + k * 1024 + j * 128 + r(128))
    add("AV", av + r(512))
    add("CDV", np.concatenate([cv + r(128), dv + r(128)]))
    allc = np.concatenate(cols)
    return off, allc

WF_OFF, WF_COLS = _wf_layout()
NWF = len(WF_COLS)


class _Rec:
    def __getattr__(self, name):
        def f(*a, **k):
            self.call = (name, a, k)
            return self
        return f


def _cap(fn):
    r = _Rec()
    fn(r)
    name, a, k = r.call
    return lambda E: getattr(E, name)(*a, **k)


class Sched:
    CE = ("pe", "act", "dve", "pool")
    ND = 12

    def __init__(self, nc):
        self.nc = nc
        self.prog = {e: [] for e in ("pe", "act", "dve", "pool", "sp")}
        self.cnt = {e: 0 for e in self.CE}
        self.sem = {e: nc.alloc_semaphore(f"sem_{e}") for e in self.CE}
        self.waited = {}
        self.lastw = {}
        self.rds = {}
        self.dsem = {q: [nc.alloc_semaphore(f"dsem_{q}{i}") for i in range(self.ND)] for q in ("sp", "pool")}
        self.duse = {q: [0] * self.ND for q in ("sp", "pool")}
        self.dnext = {"sp": 0, "pool": 0}
        self.nsem = 0
        self.cctoks = []

    def _deps(self, reads, writes):
        deps = []
        for k in reads:
            t = self.lastw.get(k)
            if t is not None:
                deps.append(t)
        for k in writes:
            t = self.lastw.get(k)
            if t is not None:
                deps.append(t)
            r = self.rds.get(k)
            if r:
                deps.extend(r["c"].values())
                deps.extend(r["d"])
        return deps

    def _mark(self, tok, reads, writes):
        for k in reads:
            r = self.rds.setdefault(k, {"c": {}, "d": []})
            if tok[0] == "c":
                r["c"][tok[1]] = tok
            else:
                r["d"].append(tok)
        for k in writes:
            self.lastw[k] = tok
            self.rds[k] = {"c": {}, "d": []}

    def _waits(self, eng, deps):
        p = self.prog[eng]
        for t in deps:
            if t[0] == "c":
                _, e2, c = t
                if e2 == eng and eng == "pe":
                    continue
                key = (eng, e2)
                if self.waited.get(key, 0) >= c:
                    continue
                self.waited[key] = c
                s = self.sem[e2]
                p.append(lambda E, s=s, c=c: E.wait_ge(s, c))
            else:
                _, s, v, sid = t
                key = (eng, sid)
                if self.waited.get(key, 0) >= v:
                    continue
                self.waited[key] = v
                p.append(lambda E, s=s, v=v: E.wait_ge(s, v))

    def op(self, eng, fn, reads=(), writes=()):
        self._waits(eng, self._deps(reads, writes))
        self.cnt[eng] += 1
        s = self.sem[eng]
        fn = _cap(fn)
        self.prog[eng].append(lambda E, fn=fn, s=s: fn(E).then_inc(s, 1))
        tok = ("c", eng, self.cnt[eng])
        self._mark(tok, reads, writes)
        return tok

    def dma(self, q, fn, reads=(), writes=()):
        self._waits(q, self._deps(reads, writes))
        i = self.dnext[q] % self.ND
        self.dnext[q] += 1
        s = self.dsem[q][i]
        sid = (q, i)
        if self.duse[q][i] > 0:
            self._waits(q, [("d", s, 16 * self.duse[q][i], sid)])
        self.duse[q][i] += 1
        fn = _cap(fn)
        self.prog[q].append(lambda E, fn=fn, s=s: fn(E).then_inc(s, 16))
        tok = ("d", s, 16 * self.duse[q][i], sid)
        self._mark(tok, reads, writes)
        return tok

    def collective(self, fn, reads=(), writes=()):
        self._waits("pool", self._deps(reads, writes))
        s = self.nc.alloc_semaphore(f"ccsem{self.nsem}")
        self.nsem += 1
        fn = _cap(fn)
        self.prog["pool"].append(lambda E, fn=fn, s=s: fn(E).then_inc(s))
        tok = ("d", s, 1, ("cc", self.nsem))
        self.cctoks.append(tok)
        self._mark(tok, reads, writes)
        return tok

    def barrier(self):
        toks = [("c", e, self.cnt[e]) for e in self.CE if self.cnt[e] > 0]
        for q in ("sp", "pool"):
            for i in range(self.ND):
                if self.duse[q][i] > 0:
                    toks.append(("d", self.dsem[q][i], 16 * self.duse[q][i], (q, i)))
        toks += self.cctoks
        for e in ("pe", "act", "dve", "pool", "sp"):
            self._waits(e, toks)

    def wait_all(self, eng, keys):
        self._waits(eng, self._deps(keys, ()))

    def emit(self):
        nc = self.nc
        with nc.Block() as block:
            @block.tensor
            def _(E):
                for f in self.prog["pe"]:
                    f(E)

            @block.scalar
            def _(E):
                for f in self.prog["act"]:
                    f(E)

            @block.vector
            def _(E):
                for f in self.prog["dve"]:
                    f(E)

            @block.gpsimd
            def _(E):
                for f in self.prog["pool"]:
                    f(E)

            @block.sync
            def _(E):
                for f in self.prog["sp"]:
                    f(E)


class Builder:
    def __init__(self, nlayers=NL, dbg=None, stop=None):
        self.stop = stop
        self.nl = nlayers
        self.dbg = dbg or []
        nc = bass.Bass("TRN2", target_bir_lowering=False)
        self.nc = nc
        self.S = Sched(nc)
        self.uid = 0
        self._decl_dram()
        self._alloc_sbuf()
        self.build()
        self.S.emit()

    def _decl_dram(self):
        nc = self.nc
        I = lambda n, s, dt=F32: nc.dram_tensor(n, s, dt, kind="ExternalInput").ap()
        self.x_in = I("x", [NT, D])
        self.ctx_in = I("ctx", [NC, D])
        self.cT = I("cT", [128, 8, 2])
        self.wmod = I("w_mod", [NL, D, 6 * D])
        self.bmodT = I("bmodT", [NL, 128, 48, 2])
        self.wf = I("wf", [NL, D, NWF])
        self.wuq = I("wuq", [NL, 256, 768])
        self.wuqs = I("wuqs", [NL, 256, 768])
        self.wkn = I("wkn", [NL, 256, 512])
        self.wvb = I("wvb", [NL, 256, 512])
        self.wbr = I("w_branch", [NL, 4, 512, D])
        self.wout = I("w_out", [NL, D, D])
        self.wup = I("w_up", [NL, D, 2 * DFF])
        self.wdn = I("w_down", [NL, DFF, D])
        self.pp = I("pp", [128, NPP])
        self.rope64 = I("rope64", [2, 128, TOK])
        self.rope32 = I("rope32", [2, 96, TOK])
        self.dmask = I("dmask", [10, 128, 512])
        self.hflag = I("hflag", [2, 128, 4, 8])
        self.consts = I("consts", [4, 128, 128])
        self.y_out = nc.dram_tensor("y", [NT, D], F32, kind="ExternalOutput").ap()
        T = lambda n, s, dt=BF16: nc.dram_tensor(n, s, dt).ap()
        self.cg = {"KA0": (256, NT), "KA1": (256, NT), "KB0": (192, NT), "KB1": (192, NT), "KB2": (192, NT), "KB3": (192, NT),
                   "KCD": (256, NT), "VA0": (NT, 256), "VA1": (NT, 256), "VB0": (NT, 256), "VB1": (NT, 256), "VCD": (NT, 256)}
        self.sendg = {g: T("send_" + g, [r, c]) for g, (r, c) in self.cg.items()}
        self.recvg = {g: T("recv_" + g, [4 * r, c]) for g, (r, c) in self.cg.items()}
        self.ktc = T("ktc", [KTROWS, NC])
        self.vc = T("vc", [NC, VW])
        self.kdl = T("kdl", [128, NT])
        self.vdl = T("vdl", [NT, 128])
        self.qs = T("qs", [12 * 128 + 8 * 96, TOK])
        self.gs = T("gs", [32 * 128, TOK])
        self.os_ = T("os", [16 * 128, TOK])
        self.send2 = T("send2", [128, 16])
        self.recv2 = T("recv2", [4 * 128, 16])
        self.dbg_out = {}
        for name, shape, dt in self.dbg:
            self.dbg_out[name] = nc.dram_tensor("dbg_" + name, shape, F32, kind="ExternalOutput").ap()

    def _alloc_sbuf(self):
        nc = self.nc
        self.soff = 16512
        def P(name, shape, dt):
            nb = int(np.prod(shape[1:])) * (4 if dt == F32 else 2)
            nb = (nb + 31) // 32 * 32
            t = nc.alloc_sbuf_tensor_at(name, shape, dt, offset=self.soff)
            self.soff += nb
            return t.ap()
        self.xT = P("xT", [128, 8, TOK], F32)
        self.modT = P("modT", [128, NL, 48, 2], F32)
        self.ident_f = P("ident_f", [128, 128], F32)
        self.ones_f = P("ones_f", [128, 128], F32)
        self.blk_f = P("blk_f", [128, 128], F32)
        self.shift_f = P("shift_f", [128, 128], F32)
        self.ident_b = P("ident_b", [128, 128], BF16)
        self.ones_b = P("ones_b", [128, 128], BF16)
        self.ppt = P("ppt", [128, NPP], F32)
        self.lamt = P("lamt", [128, NL, 8], F32)
        self.sinkrow = P("sinkrow", [128, 2, 512], F32)
        self.scT = P("scT", [128, 8, 2], F32)
        self.arena0 = self.soff
        assert self.arena0 < 105000, self.arena0

    def A(self, name, shape, dt, off):
        self.uid += 1
        nb = int(np.prod(shape[1:])) * (4 if dt == F32 else 2)
        assert self.arena0 + off + nb <= 229344, (name, self.arena0 + off + nb)
        return self.nc.alloc_sbuf_tensor_at(f"{name}_{self.uid}", shape, dt, offset=self.arena0 + off).ap()

    def dump(self, name, src_ap, keys):
        if name in self.dbg_out:
            o = self.dbg_out[name]
            if len(src_ap.shape) == 1:
                src_ap = src_ap.rearrange("(r c) -> r c", c=2048)
            if len(src_ap.shape) == 3:
                views = [(o[:, a, :], src_ap[:, a, :]) for a in range(src_ap.shape[1])]
            else:
                views = [(o, src_ap)]
            i = 0
            for (ov, sv) in views:
                ncol = sv.shape[1]
                for c0 in range(0, ncol, 1024):
                    c1 = min(ncol, c0 + 1024)
                    key = ("dbg", name, i)
                    i += 1
                    self.S.dma("pool", lambda E: E.dma_start(out=ov[:, c0:c1], in_=sv[:, c0:c1]), reads=keys, writes=[key])
                    self.outkeys.append(key)

    def build(self):
        S = self.S
        nc = self.nc
        self.outkeys = []
        self.psall = nc.alloc_psum_tensor("psall", [128, 4096], F32).ap()
        self.ps = [self.psall[:, i * 512:(i + 1) * 512] for i in range(8)]
        self.phase0()
        for l in range(self.nl):
            self.layer(l)
        S.barrier()
        self.final()
        S.wait_all("sp", self.outkeys)

    def phase0(self):
        S = self.S
        cst = [self.ident_f, self.ones_f, self.blk_f, self.shift_f]
        for i, t in enumerate(cst):
            S.dma("sp", lambda E, i=i, t=t: E.dma_start(out=t, in_=self.consts[i]), writes=[("c", i)])
        S.dma("pool", lambda E: E.dma_start(out=self.ident_b, in_=self.consts[0]), writes=[("c", "ib")])
        S.dma("pool", lambda E: E.dma_start(out=self.ones_b, in_=self.consts[1]), writes=[("c", "ob")])
        S.dma("sp", lambda E: E.dma_start(out=self.ppt, in_=self.pp), writes=["pp"])
        S.dma("sp", lambda E: E.dma_start(out=self.scT, in_=self.cT), writes=["scT"])
        S.op("act", lambda E: E.activation(out=self.scT, in_=self.scT, func=AF.Silu), reads=["scT"], writes=["scT"])
        xst = [self.A("xst", [128, D], F32, i * 4096) for i in range(2)]
        for tt in range(18):
            st = xst[tt % 2]
            src = self.x_in[tt * 128:(tt + 1) * 128, :] if tt < 16 else self.ctx_in[(tt - 16) * 128:(tt - 15) * 128, :]
            S.dma("sp", lambda E, st=st, src=src: E.dma_start(out=st, in_=src), writes=[("xst", tt % 2)])
            for half in range(2):
                bank = (tt * 2 + half) % 4
                ps = self.ps[bank]
                for q in range(4):
                    fc = half * 4 + q
                    S.op("pe", lambda E, ps=ps, q=q, st=st, fc=fc: E.transpose(
                        out=ps[:, q * 128:(q + 1) * 128], in_=st[:, fc * 128:(fc + 1) * 128], identity=self.ident_f),
                        reads=[("xst", tt % 2), ("c", 0)], writes=[("ps", bank)])
                dst = self.xT[:, half * 4:(half + 1) * 4, tt * 128:(tt + 1) * 128]
                src2 = ps.rearrange("p (a b) -> p a b", a=4)
                eng = "act" if half == 0 else "dve"
                if eng == "act":
                    S.op("act", lambda E, dst=dst, src2=src2: E.activation(out=dst, in_=src2, func=AF.Identity),
                         reads=[("ps", bank)], writes=[("xT", c) for c in range(5)])
                else:
                    S.op("dve", lambda E, dst=dst, src2=src2: E.tensor_copy(out=dst, in_=src2),
                         reads=[("ps", bank)], writes=[("xT", c) for c in range(5)])
        wm = [self.A("wm", [128, 8, 512], F32, 8192 + i * 16384) for i in range(2)]
        bm = self.A("bm", [128, 48, 2], F32, 8192 + 2 * 16384)
        n = 0
        for l in range(self.nl):
            S.dma("sp", lambda E, l=l: E.dma_start(out=bm, in_=self.bmodT[l]), writes=["bm"])
            for blk in range(12):
                w = wm[n % 2]
                S.dma("sp" if n % 2 == 0 else "pool", lambda E, w=w, l=l, blk=blk: E.dma_start(
                    out=w, in_=self.wmod[l, :, blk * 512:(blk + 1) * 512].rearrange("(kc p) c -> p kc c", p=128)),
                    writes=[("wm", n % 2)])
                bank = 4 + n % 2
                ps = self.ps[bank]
                for c4 in range(4):
                    for kc in range(8):
                        S.op("pe", lambda E, ps=ps, c4=c4, kc=kc, w=w: E.matmul(
                            ps[:, c4 * 2:c4 * 2 + 2], lhsT=w[:, kc, c4 * 128:(c4 + 1) * 128], rhs=self.scT[:, kc, :],
                            start=(kc == 0), stop=(kc == 7)),
                            reads=[("wm", n % 2), "scT"], writes=[("ps", bank)])
                dst = self.modT[:, l, blk * 4:(blk + 1) * 4, :]
                S.op("dve", lambda E, dst=dst, ps=ps, blk=blk: E.tensor_tensor(
                    out=dst, in0=ps[:, 0:8].rearrange("p (a b) -> p a b", a=4), in1=bm[:, blk * 4:(blk + 1) * 4, :], op=ALU.add),
                    reads=[("ps", bank), "bm"], writes=["modT"])
                n += 1
            for kind in (1, 4):
                dst = self.modT[:, l, kind * 8:(kind + 1) * 8, :]
                S.op("dve", lambda E, dst=dst: E.tensor_scalar(out=dst, in0=dst, scalar1=1.0, scalar2=None, op0=ALU.add),
                     reads=["modT"], writes=["modT"])

    def mod(self, l, kind, chunk, j):
        return self.modT[:, l, kind * 8 + chunk, j:j + 1]

    def rstd_cols(self, src_fn, nchunks, n, inv_dim, sq_t, rstd_t, bank, keys_r, lhs=None, lhs_key=("c", 1)):
        S = self.S
        ps = self.ps[bank]
        lhs = self.ones_f if lhs is None else lhs
        for kc in range(nchunks):
            sq = sq_t[kc % len(sq_t)]
            sqk = ("sq", id(sq_t), kc % len(sq_t))
            S.op("act", lambda E, sq=sq, kc=kc: E.activation(out=sq[:, :n], in_=src_fn(kc), func=AF.Square),
                 reads=keys_r, writes=[sqk])
            S.op("pe", lambda E, sq=sq, kc=kc: E.matmul(ps[:, :n], lhsT=lhs, rhs=sq[:, :n], start=(kc == 0), stop=(kc == nchunks - 1)),
                 reads=[sqk, lhs_key], writes=[("ps", bank)])
        rk = ("rstd", id(rstd_t))
        S.op("act", lambda E: E.activation(out=rstd_t[:, :n], in_=ps[:, :n], func=AF.Sqrt, scale=inv_dim, bias=self.ppt[:, PP_EPS:PP_EPS + 1]),
             reads=[("ps", bank), "pp"], writes=[rk])
        S.op("dve", lambda E: E.reciprocal(out=rstd_t[:, :n], in_=rstd_t[:, :n]), reads=[rk], writes=[rk])
        return rk

    def final(self):
        S = self.S
        sq_t = [self.A("fsq", [128, 512], F32, i * 2048) for i in range(2)]
        rstd = self.A("frs", [128, 512], F32, 4096)
        yT = self.A("fy", [128, 8, 512], F32, 8192)
        ost = [self.A("fo", [128, D], F32, 8192 + 16384 + i * 4096) for i in range(2)]
        for ci in range(4):
            t0, n = CH[ci]
            rk = self.rstd_cols(lambda kc: self.xT[:, kc, t0:t0 + n], 8, n, 1.0 / D, sq_t, rstd, 4, [("xT", ci)])
            for kc in range(8):
                S.op("dve", lambda E, kc=kc: E.scalar_tensor_tensor(
                    out=yT[:, kc, :n], in0=self.xT[:, kc, t0:t0 + n], scalar=self.ppt[:, PP_FN + kc:PP_FN + kc + 1],
                    in1=rstd[:, :n], op0=ALU.mult, op1=ALU.mult),
                    reads=[("xT", ci), rk, "pp"], writes=[("fy", kc)])
            for ti in range(4):
                tt = ci * 4 + ti
                o = ost[tt % 2]
                for half in range(2):
                    bank = (tt * 2 + half) % 4
                    ps = self.ps[bank]
                    for q in range(4):
                        kc = half * 4 + q
                        S.op("pe", lambda E, ps=ps, q=q, kc=kc, ti=ti: E.transpose(
                            out=ps[:, q * 128:(q + 1) * 128], in_=yT[:, kc, ti * 128:(ti + 1) * 128], identity=self.ident_f),
                            reads=[("fy", kc), ("c", 0)], writes=[("ps", bank)])
                    dst = o[:, half * 512:(half + 1) * 512]
                    if half == 0:
                        S.op("act", lambda E, dst=dst, ps=ps: E.activation(out=dst, in_=ps, func=AF.Identity),
                             reads=[("ps", bank)], writes=[("fo", tt % 2, half)])
                    else:
                        S.op("dve", lambda E, dst=dst, ps=ps: E.tensor_copy(out=dst, in_=ps),
                             reads=[("ps", bank)], writes=[("fo", tt % 2, half)])
                S.dma("sp", lambda E, o=o, tt=tt: E.dma_start(out=self.y_out[tt * 128:(tt + 1) * 128, :], in_=o),
                      reads=[("fo", tt % 2, 0), ("fo", tt % 2, 1)], writes=[("y", tt)])
                self.outkeys.append(("y", tt))

    def layer(self, l):
        last = (l == NL - 1)
        self.S.barrier()
        self.prep_layer(l)
        self.p1(l, last)
        if self.stop in ("p1a", "p1k", "p1v", "p1b", "p1g"):
            return
        self.dump(f"qs{l}", self.qs, [("qs", r0, ci) for r0 in list(range(0, 1536, 128)) + [1536 + h * 96 for h in range(8)] for ci in (range(4) if last else range(5))])
        if self.stop == "p1":
            return
        self.S.barrier()
        self.p2(l, last)
        self.dump(f"os{l}", self.os_, [k for ci in (range(4) if last else range(5)) for k in self.os_keys(ci)])
        if self.stop == "p2":
            return
        self.S.barrier()
        self.p3(l, last)
        if self.stop == "p3":
            return
        self.S.barrier()
        self.p4(l, last)

    def ppc(self, l, name, i=0):
        c = PL0 + l * PLN + PLO[name] + i
        return self.ppt[:, c:c + 1]

    def prep_layer(self, l):
        S = self.S
        lam = self.lamt
        base = PL0 + l * PLN + PLO["lam"]
        tmp = self.A("lamtmp", [128, 64], F32, 0)
        for i in range(2):
            S.op("dve", lambda E: E.tensor_tensor(out=tmp, in0=self.ppt[:, base + i * 128:base + i * 128 + 64],
                                                  in1=self.ppt[:, base + i * 128 + 64:base + i * 128 + 128], op=ALU.mult),
                 reads=["pp"], writes=["lamtmp"])
            S.op("dve", lambda E: E.tensor_reduce(out=lam[:, l, i:i + 1], in_=tmp, axis=mybir.AxisListType.X, op=ALU.add),
                 reads=["lamtmp"], writes=["lamt"])
        S.op("act", lambda E: E.activation(out=lam[:, l, 0:2], in_=lam[:, l, 0:2], func=AF.Exp), reads=["lamt"], writes=["lamt"])
        S.op("dve", lambda E: E.tensor_tensor(out=lam[:, l, 2:3], in0=lam[:, l, 0:1], in1=lam[:, l, 1:2], op=ALU.subtract),
             reads=["lamt"], writes=["lamt"])
        lam_init = 0.8 - 0.6 * math.exp(-0.3 * l)
        S.op("dve", lambda E: E.tensor_scalar(out=lam[:, l, 3:4], in0=lam[:, l, 2:3], scalar1=-1.0, scalar2=-lam_init,
                                              op0=ALU.mult, op1=ALU.add), reads=["lamt"], writes=["lamt"])
        sb = PL0 + l * PLN + PLO["sink"]
        sk = self.A("sinkexp", [128, 8], F32, 512)
        S.op("act", lambda E: E.activation(out=sk, in_=self.ppt[:, sb:sb + 8], func=AF.Exp), reads=["pp"], writes=["sinkexp"])
        for j in range(2):
            for g in range(4):
                S.op("dve", lambda E: E.tensor_scalar(out=self.sinkrow[:, j, g * 128:(g + 1) * 128], in0=self.ones_f,
                                                      scalar1=sk[:, 4 * j + g:4 * j + g + 1], scalar2=None, op0=ALU.mult),
                     reads=["sinkexp", ("c", 1)], writes=["sinkrow"])

    def modulate(self, l, ksh, ksc, dst, dcol, dkey, chunks, o_sq, o_rstd, o_tmp):
        S = self.S
        sq_t = [self.A("msq", [128, 512], F32, o_sq + i * 2048) for i in range(2)]
        rstd = self.A("mrs", [128, 512], F32, o_rstd)
        tmp = [self.A("mtmp", [128, 512], F32, o_tmp + i * 2048) for i in range(2)]
        for ci in chunks:
            t0, n = CH[ci]
            j = 1 if ci == 4 else 0
            rk = self.rstd_cols(lambda kc: self.xT[:, kc, t0:t0 + n], 8, n, 1.0 / D, sq_t, rstd, 7, [("xT", ci)])
            d0 = dcol(ci)
            for kc in range(8):
                t = tmp[kc % 2]
                tk = ("mtmp", o_tmp, kc % 2)
                S.op("dve", lambda E: E.tensor_tensor(out=t[:, :n], in0=self.xT[:, kc, t0:t0 + n], in1=rstd[:, :n], op=ALU.mult),
                     reads=[("xT", ci), rk], writes=[tk])
                S.op("act", lambda E: E.activation(out=dst[:, kc, d0:d0 + n], in_=t[:, :n], func=AF.Identity,
                                                   scale=self.mod(l, ksc, kc, j), bias=self.mod(l, ksh, kc, j)),
                     reads=[tk, "modT"], writes=[(dkey, ci)])

    def p1(self, l, last):
        S = self.S
        hT = self.A("hT", [128, 8, TOK], BF16, 0)
        ropeC = self.A("ropeC", [128, TOK], F32, 36864)
        ropeS = self.A("ropeS", [128, TOK], F32, 46080)
        wt = [self.A("wt", [128, 8, 256], BF16, 55296 + i * 4096) for i in range(2)]
        T = [self.A("p1t", [128, 512], F32, 63488 + i * 2048) for i in range(6)]
        sqt = [self.A("p1sq", [128, 512], F32, 75776 + i * 2048) for i in range(2)]
        rst = self.A("p1rs", [128, 512], F32, 79872)
        lat = self.A("lat", [128, 4, 512], F32, 81920)
        latn = self.A("latn", [128, 4, 512], BF16, 90112)
        ost = [self.A("p1o", [128, 512], BF16, 94208 + i * 1024) for i in range(4)]
        wuq = self.A("wuq", [128, 2, 768], BF16, 98304)
        wuqs = self.A("wuqs", [128, 2, 768], BF16, 101376)
        wkn = self.A("wkn", [128, 2, 512], BF16, 104448)
        wvb = self.A("wvb", [128, 2, 512], BF16, 106496)
        KRt = self.A("KRt", [128, 512], BF16, 108544)
        wv = self.A("wv", [128, 8, 512], BF16, 109568)
        rB = [self.A("rB", [128, 512], F32, 117760 + i * 2048) for i in range(2)]
        HK = [("hT", ci) for ci in range(5)]

        S.dma("sp", lambda E: E.dma_start(out=ropeC, in_=self.rope64[0]), writes=["ropeC"])
        S.dma("sp", lambda E: E.dma_start(out=ropeS, in_=self.rope64[1]), writes=["ropeS"])
        for t, src, k in ((wuq, self.wuq, "wuq"), (wuqs, self.wuqs, "wuqs"), (wkn, self.wkn, "wkn"), (wvb, self.wvb, "wvb")):
            S.dma("pool", lambda E: E.dma_start(out=t, in_=src[l].rearrange("(b p) c -> p b c", p=128)), writes=[k])
        self.modulate(l, 0, 1, hT, lambda ci: CH[ci][0], "hT", range(5), 75776, 79872, 63488)
        self.dump(f"hT{l}", hT, HK)
        S.barrier()
        if self.stop == "p1a":
            return

        self.sendkeys = {g: [] for g in self.cg}
        cnt = [0]
        wcnt = [0]
        ocnt = [0]

        gq = []

        def gather_later(g):
            gq.append([2, g])

        def gq_tick(force=False):
            for it in gq:
                it[0] -= 1
            while gq and (force or gq[0][0] <= 0):
                self.gather(gq.pop(0)[1])

        def load_w(name, ncols=256):
            gq_tick()
            i = wcnt[0] % 2
            wcnt[0] += 1
            o = WF_OFF[name]
            S.dma("pool", lambda E: E.dma_start(out=wt[i][:, :, 0:ncols],
                                                in_=self.wf[l, :, o:o + ncols].rearrange("(kc p) c -> p kc c", p=128)),
                  writes=[("wt", i)])
            return wt[i], ("wt", i)

        def mm8(bank, w, wk, c0, ncols, ci, nrows=128):
            t0, n = CH[ci]
            ps = self.ps[bank]
            for kc in range(8):
                S.op("pe", lambda E: E.matmul(ps[0:nrows, :n], lhsT=w[:, kc, c0:c0 + nrows], rhs=hT[:, kc, t0:t0 + n],
                                              start=(kc == 0), stop=(kc == 7)),
                     reads=[wk, ("hT", ci)], writes=[("ps", bank)])

        def store(o, ok, nrows, n, dst, dkeys, q="sp"):
            S.dma(q, lambda E: E.dma_start(out=dst, in_=o[0:nrows, :n]), reads=[ok], writes=dkeys)

        def rope_block(name, ci, dsts, norm=None):
            w, wk = name
            t0, n = CH[ci]
            c = cnt[0]
            cnt[0] += 1
            ba, bb = (c % 2) * 2, (c % 2) * 2 + 1
            mm8(ba, w, wk, 0, 128, ci)
            mm8(bb, w, wk, 128, 128, ci)
            pa, pb = self.ps[ba], self.ps[bb]
            u1, u2 = T[(c % 2) * 3], T[(c % 2) * 3 + 1]
            k1, k2 = ("p1t", (c % 2) * 3), ("p1t", (c % 2) * 3 + 1)
            oi = ocnt[0] % 4
            ocnt[0] += 1
            o, ok = ost[oi], ("p1o", oi)
            if norm is None:
                S.op("dve", lambda E: E.tensor_tensor(out=u1[:, :n], in0=pa[:, :n], in1=ropeC[:, t0:t0 + n], op=ALU.mult),
                     reads=[("ps", ba), "ropeC"], writes=[k1])
                S.op("dve", lambda E: E.tensor_tensor(out=u2[:, :n], in0=pb[:, :n], in1=ropeS[:, t0:t0 + n], op=ALU.mult),
                     reads=[("ps", bb), "ropeS"], writes=[k2])
                S.op("dve", lambda E: E.tensor_tensor(out=o[:, :n], in0=u1[:, :n], in1=u2[:, :n], op=ALU.add),
                     reads=[k1, k2], writes=[ok])
            else:
                g, gs = norm
                sq = T[(c % 2) * 3 + 2]
                k3 = ("p1t", (c % 2) * 3 + 2)
                bn = 4 + c % 2
                S.op("act", lambda E: E.activation(out=sq[:, :n], in_=pa[:, :n], func=AF.Square), reads=[("ps", ba)], writes=[k3])
                S.op("pe", lambda E: E.matmul(self.ps[bn][:, :n], lhsT=self.blk_f, rhs=sq[:, :n], start=True, stop=True),
                     reads=[k3, ("c", 2)], writes=[("ps", bn)])
                S.op("act", lambda E: E.activation(out=sq[:, :n], in_=self.ps[bn][:, :n], func=AF.Sqrt, scale=1.0 / 64,
                                                   bias=self.ppt[:, PP_EPS:PP_EPS + 1]), reads=[("ps", bn), "pp"], writes=[k3])
                S.op("dve", lambda E: E.reciprocal(out=sq[:, :n], in_=sq[:, :n]), reads=[k3], writes=[k3])
                S.op("dve", lambda E: E.scalar_tensor_tensor(out=u1[:, :n], in0=pa[:, :n], scalar=g, in1=ropeC[:, t0:t0 + n],
                                                             op0=ALU.mult, op1=ALU.mult), reads=[("ps", ba), "ropeC", "pp"], writes=[k1])
                S.op("dve", lambda E: E.scalar_tensor_tensor(out=u2[:, :n], in0=pb[:, :n], scalar=gs, in1=ropeS[:, t0:t0 + n],
                                                             op0=ALU.mult, op1=ALU.mult), reads=[("ps", bb), "ropeS", "pp"], writes=[k2])
                S.op("dve", lambda E: E.tensor_tensor(out=u1[:, :n], in0=u1[:, :n], in1=u2[:, :n], op=ALU.add),
                     reads=[k1, k2], writes=[k1])
                S.op("dve", lambda E: E.tensor_tensor(out=o[:, :n], in0=u1[:, :n], in1=sq[:, :n], op=ALU.mult),
                     reads=[k1, k3], writes=[ok])
            for dst, dk in dsts:
                store(o, ok, 128, n, dst, dk)

        def kdst(row0, ci, extra=None):
            t0, n = CH[ci]
            if ci < 4:
                g, lr = self.kt_loc(row0)
                d = [(self.sendg[g][lr:lr + 128, t0:t0 + n], [("send", row0, ci)])]
                self.sendkeys[g].append(("send", row0, ci))
                if extra is not None:
                    d.append((extra[:, t0:t0 + n], [("kdl", ci)]))
            else:
                d = [(self.ktc[row0:row0 + 128, :], [("ktc", row0)])]
            return d

        for h in range(4):
            w = load_w(f"AK{h}")
            for ci in range(5):
                rope_block(w, ci, kdst(h * 128, ci))
        w = load_w("CK")
        for ci in range(5):
            rope_block(w, ci, kdst(1280, ci), norm=(self.ppc(l, "ckg"), self.ppc(l, "ckgs")))
        w = load_w("DK")
        for ci in range(5):
            rope_block(w, ci, kdst(1408, ci, extra=self.kdl))
        if self.stop == "p1k":
            return

        vst = [self.A("vst", [128, 512], BF16, 121856 + i * 1024) for i in range(2)]
        vcnt = [0]

        def vproj(ncols, wcol_name, dcol0, also_vdl=False):
            o = WF_OFF[wcol_name]
            S.dma("pool", lambda E: E.dma_start(out=wv[:, :, 0:ncols], in_=self.wf[l, :, o:o + ncols].rearrange("(kc p) c -> p kc c", p=128)),
                  writes=["wv"])
            for tt in range(18):
                ci = min(tt // 4, 4)
                c = vcnt[0]
                vcnt[0] += 1
                bank = 4 + c % 2
                ps = self.ps[bank]
                for kc in range(8):
                    S.op("pe", lambda E: E.matmul(ps[:, :ncols], lhsT=hT[:, kc, tt * 128:(tt + 1) * 128], rhs=wv[:, kc, 0:ncols],
                                                  start=(kc == 0), stop=(kc == 7)), reads=["wv", ("hT", ci)], writes=[("ps", bank)])
                v = vst[c % 2]
                vk = ("vst", c % 2)
                if c % 2 == 0:
                    S.op("act", lambda E: E.activation(out=v[:, :ncols], in_=ps[:, :ncols], func=AF.Identity), reads=[("ps", bank)], writes=[vk])
                else:
                    S.op("dve", lambda E: E.tensor_copy(out=v[:, :ncols], in_=ps[:, :ncols]), reads=[("ps", bank)], writes=[vk])
                if tt < 16:
                    for cc0 in range(0, ncols, 256):
                        g, lc = self.v_loc(dcol0 + cc0)
                        key = ("sendv", g, tt)
                        self.sendkeys[g].append(key)
                        S.dma("sp", lambda E: E.dma_start(out=self.sendg[g][tt * 128:(tt + 1) * 128, lc:lc + 256], in_=v[:, cc0:cc0 + 256]),
                              reads=[vk], writes=[key])
                    if also_vdl:
                        S.dma("sp", lambda E: E.dma_start(out=self.vdl[tt * 128:(tt + 1) * 128, :], in_=v[:, 128:256]),
                              reads=[vk], writes=[("vdl", tt)])
                else:
                    S.dma("sp", lambda E: E.dma_start(out=self.vc[(tt - 16) * 128:(tt - 15) * 128, dcol0:dcol0 + ncols], in_=v[:, :ncols]),
                          reads=[vk], writes=[("vc", wcol_name, tt)])
        self.late = ["KB0", "KB1", "KB2", "KB3", "VB0", "VB1", "KCD", "VCD"]
        vproj(512, "AV", 0)
        vproj(256, "CDV", 1024, also_vdl=True)
        if self.stop == "p1v":
            return

        wBC, wBCk = load_w("BC0")
        wKR, wKRk = load_w("KR")

        def load_rope32(ci):
            t0, n = CH[ci]
            for i in range(2):
                S.dma("sp", lambda E: E.dma_start(out=rB[i][0:96, :n], in_=self.rope32[i, :, t0:t0 + n]), writes=[("rB", i)])

        def latent_norm(w, wk, ci, li0, gname):
            t0, n = CH[ci]
            mm8(0, w, wk, 0, 128, ci)
            mm8(1, w, wk, 128, 128, ci)
            S.op("act", lambda E: E.activation(out=lat[:, li0, :n], in_=self.ps[0][:, :n], func=AF.Identity),
                 reads=[("ps", 0)], writes=[("lat", li0)])
            S.op("dve", lambda E: E.tensor_copy(out=lat[:, li0 + 1, :n], in_=self.ps[1][:, :n]), reads=[("ps", 1)], writes=[("lat", li0 + 1)])
            rk = self.rstd_cols(lambda kc: lat[:, li0 + kc, :n], 2, n, 1.0 / 256, sqt, rst, 6, [("lat", li0), ("lat", li0 + 1)])
            for b in range(2):
                S.op("dve", lambda E: E.scalar_tensor_tensor(out=latn[:, li0 + b, :n], in0=lat[:, li0 + b, :n], scalar=self.ppc(l, gname, b),
                                                             in1=rst[:, :n], op0=ALU.mult, op1=ALU.mult),
                     reads=[("lat", li0 + b), rk, "pp"], writes=[("latn", li0 + b)])

        send_kB = 512
        for ci in range(5):
            t0, n = CH[ci]
            load_rope32(ci)
            latent_norm(wBC, wBCk, ci, 2, "kvn")
            mm8(2, wKR, wKRk, 0, 128, ci)
            mm8(3, wKR, wKRk, 128, 128, ci)
            S.op("dve", lambda E: E.tensor_tensor(out=T[0][0:96, :n], in0=self.ps[2][0:96, :n], in1=rB[0][0:96, :n], op=ALU.mult),
                 reads=[("ps", 2), ("rB", 0)], writes=[("p1t", 0)])
            S.op("dve", lambda E: E.tensor_tensor(out=T[1][0:96, :n], in0=self.ps[3][0:96, :n], in1=rB[1][0:96, :n], op=ALU.mult),
                 reads=[("ps", 3), ("rB", 1)], writes=[("p1t", 1)])
            S.op("dve", lambda E: E.tensor_tensor(out=KRt[0:96, :n], in0=T[0][0:96, :n], in1=T[1][0:96, :n], op=ALU.add),
                 reads=[("p1t", 0), ("p1t", 1)], writes=["KRt"])
            for h in range(8):
                bank = 4 + h % 2
                ps = self.ps[bank]
                for b in range(2):
                    S.op("pe", lambda E: E.matmul(ps[0:64, :n], lhsT=wkn[:, b, h * 64:(h + 1) * 64], rhs=latn[:, 2 + b, :n],
                                                  start=(b == 0), stop=(b == 1)), reads=["wkn", ("latn", 2 + b)], writes=[("ps", bank)])
                oi = ocnt[0] % 4
                ocnt[0] += 1
                o, ok = ost[oi], ("p1o", oi)
                S.op("act", lambda E: E.activation(out=o[0:64, :n], in_=ps[0:64, :n], func=AF.Identity), reads=[("ps", bank)], writes=[ok])
                S.op("dve", lambda E: E.tensor_copy(out=o[64:96, :n], in_=KRt[64:96, :n]), reads=["KRt"], writes=[ok])
                r0 = send_kB + h * 96
                if ci < 4:
                    key = ("send", r0, ci)
                    g, lr = self.kt_loc(r0)
                    self.sendkeys[g].append(key)
                    store(o, ok, 96, n, self.sendg[g][lr:lr + 96, t0:t0 + n], [key])
                else:
                    store(o, ok, 96, n, self.ktc[r0:r0 + 96, :], [("ktc", r0)])
            for ti in range(n // 128):
                tt = t0 // 128 + ti
                c = vcnt[0]
                vcnt[0] += 1
                bank = 6 + c % 2
                ps = self.ps[bank]
                for b in range(2):
                    S.op("pe", lambda E: E.matmul(ps[:, :], lhsT=latn[:, 2 + b, ti * 128:(ti + 1) * 128], rhs=wvb[:, b, :],
                                                  start=(b == 0), stop=(b == 1)), reads=["wvb", ("latn", 2 + b)], writes=[("ps", bank)])
                v = vst[c % 2]
                vk = ("vst", c % 2)
                S.op("act", lambda E: E.activation(out=v, in_=ps, func=AF.Identity), reads=[("ps", bank)], writes=[vk])
                if tt < 16:
                    for hb in range(2):
                        g = f"VB{hb}"
                        key = ("sendv", g, tt)
                        self.sendkeys[g].append(key)
                        S.dma("sp", lambda E: E.dma_start(out=self.sendg[g][tt * 128:(tt + 1) * 128, :], in_=v[:, hb * 256:(hb + 1) * 256]),
                              reads=[vk], writes=[key])
                else:
                    S.dma("sp", lambda E: E.dma_start(out=self.vc[(tt - 16) * 128:(tt - 15) * 128, 512:1024], in_=v), reads=[vk],
                          writes=[("vc", "B", tt)])

        if self.stop == "p1b":
            return
        for g in ("KA0", "KA1", "VA0", "VA1"):
            gather_later(g)
        if self.stop == "p1g":
            return

        qchunks = range(4) if last else range(5)

        def qdst(row0, ci, nrows=128):
            t0, n = CH[ci]
            return [(self.qs[row0:row0 + nrows, t0:t0 + n], [("qs", row0, ci)])]
        for h in range(4):
            w = load_w(f"AQ{h}")
            for ci in qchunks:
                rope_block(w, ci, qdst(h * 128, ci))
        for m in range(4):
            w = load_w(f"CQ{m}")
            for ci in qchunks:
                rope_block(w, ci, qdst(512 + m * 128, ci), norm=(self.ppc(l, "cqg"), self.ppc(l, "cqgs")))
        for m in range(4):
            w = load_w(f"DQ{m}")
            for ci in qchunks:
                rope_block(w, ci, qdst(1024 + m * 128, ci))
        wBQ, wBQk = load_w("BQ0")
        for ci in qchunks:
            t0, n = CH[ci]
            load_rope32(ci)
            latent_norm(wBQ, wBQk, ci, 0, "qn")
            for h in range(8):
                ba, bb = 2, 3
                for (bank, wq, wqk) in ((ba, wuq, "wuq"), (bb, wuqs, "wuqs")):
                    for b in range(2):
                        S.op("pe", lambda E: E.matmul(self.ps[bank][0:96, :n], lhsT=wq[:, b, h * 96:(h + 1) * 96], rhs=latn[:, b, :n],
                                                      start=(b == 0), stop=(b == 1)), reads=[wqk, ("latn", b)], writes=[("ps", bank)])
                S.op("dve", lambda E: E.tensor_tensor(out=T[0][0:96, :n], in0=self.ps[ba][0:96, :n], in1=rB[0][0:96, :n], op=ALU.mult),
                     reads=[("ps", ba), ("rB", 0)], writes=[("p1t", 0)])
                S.op("dve", lambda E: E.tensor_tensor(out=T[1][0:96, :n], in0=self.ps[bb][0:96, :n], in1=rB[1][0:96, :n], op=ALU.mult),
                     reads=[("ps", bb), ("rB", 1)], writes=[("p1t", 1)])
                oi = ocnt[0] % 4
                ocnt[0] += 1
                o, ok = ost[oi], ("p1o", oi)
                S.op("dve", lambda E: E.tensor_tensor(out=o[0:96, :n], in0=T[0][0:96, :n], in1=T[1][0:96, :n], op=ALU.add),
                     reads=[("p1t", 0), ("p1t", 1)], writes=[ok])
                r0 = 1536 + h * 96
                store(o, ok, 96, n, self.qs[r0:r0 + 96, t0:t0 + n], [("qs", r0, ci)])
        for gp in range(16):
            w, wk = load_w(f"G{gp * 2}")
            for ci in qchunks:
                t0, n = CH[ci]
                for b in range(2):
                    c = cnt[0]
                    cnt[0] += 1
                    bank = c % 4
                    mm8(bank, w, wk, b * 128, 128, ci)
                    oi = ocnt[0] % 4
                    ocnt[0] += 1
                    o, ok = ost[oi], ("p1o", oi)
                    S.op("act", lambda E: E.activation(out=o[:, :n], in_=self.ps[bank][:, :n], func=AF.Sigmoid), reads=[("ps", bank)], writes=[ok])
                    gi = gp * 2 + b
                    store(o, ok, 128, n, self.gs[gi * 128:(gi + 1) * 128, t0:t0 + n], [("gs", gi, ci)])
        gq_tick(force=True)

    def p2(self, l, last):
        S = self.S
        KT = [self.A("KT", [128, NKB * 128], BF16, i * 16896) for i in range(2)]
        V = [self.A("V", [128, NKB, 128], BF16, 33792 + i * 16896) for i in range(2)]
        QT = self.A("QT", [128, 4, TOK], BF16, 67584)
        NPT = 4
        PT = [self.A("PT", [128, 2, 512], BF16, 86016 + i * 2048) for i in range(NPT)]
        T = [self.A("p2t", [128, 512], F32, 94208 + i * 2048) for i in range(8)]
        accS2 = [self.A("accS", [128, 2, 512], F32, 110592 + i * 4096) for i in range(2)]
        masks = self.A("masks", [128, 10, 512], BF16, 110592)
        ost = [self.A("p2o", [128, 512], BF16, 120832 + i * 1024) for i in range(2)]
        pend = []

        def defer(delay, fn):
            pend.append([delay, fn, st["eid"]])

        def tick():
            while pend and pend[0][0] <= 0:
                pend.pop(0)[1]()
            for p in pend:
                p[0] -= 1

        def flush():
            while pend:
                pend.pop(0)[1]()

        def flush_upto(eid):
            while pend and pend[0][2] <= eid:
                pend.pop(0)[1]()
        qch = list(range(4)) if last else list(range(5))
        for g in self.late:
            self.gather(g)
        st = {"step": 0, "acc": 0, "o": 0, "u": 0, "eid": 0}
        lam_init = 0.8 - 0.6 * math.exp(-0.3 * l)

        def load_kt(i, row0, nrows, pbase):
            S.dma("sp", lambda E: E.dma_start(out=KT[i][pbase:pbase + nrows, 0:NC], in_=self.ktc[row0:row0 + nrows, :]),
                  reads=self._ktc_keys(row0, nrows), writes=[("KT", i)])
            g, lr = self.kt_loc(row0)
            gr = self.cg[g][0]
            for r in range(4):
                S.dma("sp", lambda E: E.dma_start(out=KT[i][pbase:pbase + nrows, NC + r * NT:NC + (r + 1) * NT],
                                                  in_=self.recvg[g][r * gr + lr:r * gr + lr + nrows, :]), reads=[("recv", g)], writes=[("KT", i)])

        def load_v(i, col0, vw):
            S.dma("sp", lambda E: E.dma_start(out=V[i][:, 0:2, 0:vw], in_=self.vc[:, col0:col0 + vw].rearrange("(b p) c -> p b c", p=128)),
                  reads=self._vc_keys(), writes=[("V", i)])
            g, lc = self.v_loc(col0)
            for r in range(4):
                S.dma("sp", lambda E: E.dma_start(out=V[i][:, 2 + r * 16:2 + (r + 1) * 16, 0:vw],
                                                  in_=self.recvg[g][r * NT:(r + 1) * NT, lc:lc + vw].rearrange("(b p) c -> p b c", p=128)),
                      reads=[("recv", g)], writes=[("V", i)])

        def set_ones(i):
            S.op("pool", lambda E: E.memset(V[i][:, :, 64:128], 1.0), writes=[("V", i)])

        def pipeline(steps, N, scale, nsg=2, accS=None):
            groups = [steps[i:i + 2] for i in range(0, len(steps), 2)]
            ng = len(groups)
            base = st["step"]
            L = nsg - 1
            for gi in range(ng + L):
                tick()
                if gi < ng:
                    grp = groups[gi]
                    g = base + gi
                    b0 = (g % nsg) * 2
                    for si, sp in enumerate(grp):
                        bank = b0 + si
                        ps = self.ps[bank]
                        has_mask = sp.get("mask") is not None
                        S.op("pe", lambda E: E.matmul(ps[:, :N], lhsT=sp["lhsT"], rhs=sp["rhs"], start=True, stop=not has_mask),
                             reads=[sp["lk"], sp["rk"]], writes=[("ps", bank)])
                        if has_mask:
                            S.op("pe", lambda E: E.matmul(ps[:, :N], lhsT=self.ident_b, rhs=sp["mask"], start=False, stop=True),
                                 reads=["masks", ("c", "ib")], writes=[("ps", bank)])
                    ns = len(grp)
                    pt = PT[g % NPT]
                    src = self.psall[:, b0 * 512:(b0 + ns) * 512].rearrange("p (a n) -> p a n", a=ns)[:, :, :N]
                    S.op("act", lambda E: E.activation(out=pt[:, 0:ns, :N], in_=src, func=AF.Exp, scale=scale),
                         reads=[("ps", b0 + si) for si in range(ns)], writes=[("PT", g % NPT)])
                if gi >= L:
                    grp = groups[gi - L]
                    g = base + gi - L
                    pt = PT[g % NPT]
                    for si, sp in enumerate(grp):
                        for (bank, lhsT, key, start, stop) in sp["pv"]:
                            S.op("pe", lambda E: E.matmul(self.ps[bank][:, :N], lhsT=lhsT, rhs=pt[:, si, :N], start=start, stop=stop),
                                 reads=[key, ("PT", g % NPT)], writes=[("ps", bank)])
                    if accS is not None:
                        ac, ackey = accS
                        nd = (N * 11 // 16) // 2 * 2
                        for eng, c0, c1, kk in (("dve", 0, nd, (ackey, "d")), ("pool", nd, N, (ackey, "p"))):
                            if gi - L == 0:
                                S.op(eng, lambda E: E.tensor_copy(out=ac[:, 1, c0:c1], in_=pt[:, 1, c0:c1]), reads=[("PT", g % NPT)], writes=[kk])
                            else:
                                S.op(eng, lambda E: E.tensor_tensor(out=ac[:, 1, c0:c1], in0=ac[:, 1, c0:c1], in1=pt[:, 1, c0:c1], op=ALU.add),
                                     reads=[("PT", g % NPT), kk], writes=[kk])
            st["step"] += ng

        def epi_bcd(acc, N, dsts, sink_j=None, delays=(1, 5)):
            flush_upto(st["u"] - 2)
            st["eid"] = st["u"]
            u = st["u"] % 2
            st["u"] += 1
            oc, rc = T[u * 2], T[u * 2 + 1]
            kc_, kr_ = ("p2t", u * 2), ("p2t", u * 2 + 1)
            oi = st["o"] % 2
            st["o"] += 1
            o, ok = ost[oi], ("p2o", oi)
            S.op("act", lambda E: E.activation(out=oc[:, :N], in_=self.ps[acc][:, :N], func=AF.Identity), reads=[("ps", acc)], writes=[kc_])

            def s1():
                S.dma("sp", lambda E: E.dma_start(out=rc[0:64, :N], in_=oc[64:128, :N]), reads=[kc_], writes=[kr_])

            def s2():
                if sink_j is not None:
                    S.op("dve", lambda E: E.tensor_tensor(out=rc[0:64, :N], in0=rc[0:64, :N], in1=self.sinkrow[0:64, sink_j, :N], op=ALU.add),
                         reads=[kr_, "sinkrow"], writes=[kr_])
                S.op("dve", lambda E: E.reciprocal(out=rc[0:64, :N], in_=rc[0:64, :N]), reads=[kr_], writes=[kr_])
                S.op("dve", lambda E: E.tensor_tensor(out=o[0:64, :N], in0=oc[0:64, :N], in1=rc[0:64, :N], op=ALU.mult),
                     reads=[kc_, kr_], writes=[ok])
                for (c0, nc_, dst, keys) in dsts:
                    S.dma("sp", lambda E: E.dma_start(out=dst, in_=o[0:64, c0:c0 + nc_]), reads=[ok], writes=keys)
            defer(delays[0], s1)
            defer(delays[1], s2)

        def epi_a(N, dst, keys, accS):
            flush_upto(st["u"] - 2)
            st["eid"] = st["u"]
            ac, ackey = accS
            u = st["u"] % 2
            st["u"] += 1
            c0_, c1_, t2, t3 = T[u * 4], T[u * 4 + 1], T[u * 4 + 2], T[u * 4 + 3]
            k0, k1, k2, k3 = [("p2t", u * 4 + i) for i in range(4)]
            oi = st["o"] % 2
            st["o"] += 1
            o, ok = ost[oi], ("p2o", oi)
            S.op("dve", lambda E: E.tensor_copy(out=c0_[:, :N], in_=self.ps[6][:, :N]), reads=[("ps", 6)], writes=[k0])
            S.op("dve", lambda E: E.tensor_copy(out=c1_[:, :N], in_=self.ps[7][:, :N]), reads=[("ps", 7)], writes=[k1])

            S.op("dve", lambda E: E.tensor_copy(out=t2[:, :N], in_=self.ps[4][:, :N]), reads=[("ps", 4)], writes=[k2])

            def s1():
                S.op("pe", lambda E: E.matmul(self.ps[5][:, :N], lhsT=self.ones_f, rhs=ac[:, 1, :N], start=True, stop=True),
                     reads=[(ackey, "d"), (ackey, "p"), ("c", 1)], writes=[("ps", 5)])

            def s2a():
                S.op("dve", lambda E: E.reciprocal(out=t2[:, :N], in_=t2[:, :N]), reads=[k2], writes=[k2])

            def s2b():
                S.op("dve", lambda E: E.reciprocal(out=t3[:, :N], in_=self.ps[5][:, :N]), reads=[("ps", 5)], writes=[k3])

            def s2c():
                S.op("pool", lambda E: E.tensor_tensor(out=c0_[:, :N], in0=c0_[:, :N], in1=t2[:, :N], op=ALU.mult),
                     reads=[k0, k2], writes=[k0])
                S.op("pool", lambda E: E.tensor_tensor(out=c1_[:, :N], in0=c1_[:, :N], in1=t3[:, :N], op=ALU.mult),
                     reads=[k1, k3], writes=[k1])
                S.op("dve", lambda E: E.scalar_tensor_tensor(out=c0_[:, :N], in0=c1_[:, :N], scalar=self.lamt[:, l, 3:4], in1=c0_[:, :N],
                                                             op0=ALU.mult, op1=ALU.add), reads=[k0, k1, "lamt"], writes=[k0])

            def s2d():
                S.op("act", lambda E: E.activation(out=t2[:, :N], in_=c0_[:, :N], func=AF.Square), reads=[k0], writes=[k2])

            def s3():
                S.op("pe", lambda E: E.matmul(self.ps[5][:, :N], lhsT=self.ones_f, rhs=t2[:, :N], start=True, stop=True),
                     reads=[k2, ("c", 1)], writes=[("ps", 5)])

            def s4():
                S.op("act", lambda E: E.activation(out=t3[:, :N], in_=self.ps[5][:, :N], func=AF.Ln, scale=1.0 / 128,
                                                   bias=self.ppt[:, PP_EPS:PP_EPS + 1]), reads=[("ps", 5), "pp"], writes=[k3])
                S.op("act", lambda E: E.activation(out=t3[:, :N], in_=t3[:, :N], func=AF.Exp, scale=-0.5), reads=[k3], writes=[k3])

            def s5():
                S.op("dve", lambda E: E.scalar_tensor_tensor(out=c1_[:, :N], in0=c0_[:, :N], scalar=self.ppc(l, "subl"), in1=t3[:, :N],
                                                             op0=ALU.mult, op1=ALU.mult), reads=[k0, k3, "pp"], writes=[k1])
                S.op("dve", lambda E: E.tensor_scalar(out=o[:, :N], in0=c1_[:, :N], scalar1=1.0 - lam_init, scalar2=None, op0=ALU.mult),
                     reads=[k1], writes=[ok])
                S.dma("sp", lambda E: E.dma_start(out=dst, in_=o[:, :N]), reads=[ok], writes=keys)
            defer(2, s1)
            defer(4, s2a)
            defer(8, s2b)
            defer(11, s2c)
            defer(13, s2d)
            defer(15, s3)
            defer(18, s4)
            defer(21, s5)

        def qkeys(row0, nrows=128):
            return [("qs", row0, ci) for ci in qch]

        S.dma("sp", lambda E: E.dma_start(out=QT, in_=self.qs[0:512, :].rearrange("(b p) t -> p b t", p=128)),
              reads=[k for h in range(4) for k in qkeys(h * 128)], writes=["QT"])
        S.op("pool", lambda E: E.memset(KT[0][64:128, :], 0.0), writes=[("KT", 0)])
        S.op("pool", lambda E: E.memset(KT[1][0:64, :], 0.0), writes=[("KT", 1)])
        for h in range(4):
            i = h % 2
            load_kt(0, h * 128, 64, 0)
            load_kt(1, h * 128 + 64, 64, 64)
            load_v(i, h * 128, 128)
            for ci in qch:
                t0, N = CH[ci]
                nkb = NKB if ci < 4 else 2
                steps = []
                for kb in range(nkb):
                    for m in range(2):
                        steps.append(dict(lhsT=KT[m][:, kb * 128:(kb + 1) * 128], lk=("KT", m),
                                          rhs=QT[:, h, t0:t0 + N], rk="QT",
                                          pv=[(6 + m, V[i][:, kb, :], ("V", i), kb == 0, kb == nkb - 1)] +
                                             ([(4, self.ones_b, ("c", "ob"), kb == 0, kb == nkb - 1)] if m == 0 else [])))
                flush_upto(st["u"] - 2)
                au = st["u"]
                accS = (accS2[au % 2], ("accS", au % 2))
                pipeline(steps, N, 0.125, nsg=2, accS=accS)
                epi_a(N, self.os_[h * 128:(h + 1) * 128, t0:t0 + N], [("os", h, ci)], accS)
        flush()
        set_ones(0)
        set_ones(1)
        for h in range(8):
            i = h % 2
            if h % 4 == 0:
                S.dma("sp", lambda E: E.dma_start(out=QT[0:96, :, :], in_=self.qs[1536 + h * 96:1536 + (h + 4) * 96, :].rearrange("(b p) t -> p b t", p=96)),
                      reads=[k for hh in range(h, h + 4) for k in qkeys(1536 + hh * 96)], writes=["QT"])
            load_kt(i, 512 + h * 96, 96, 0)
            load_v(i, 512 + h * 64, 64)
            for ci in qch:
                t0, N = CH[ci]
                nkb = NKB if ci < 4 else 2
                acc = 6 + st["acc"] % 2
                st["acc"] += 1
                steps = [dict(lhsT=KT[i][0:96, kb * 128:(kb + 1) * 128], lk=("KT", i), rhs=QT[0:96, h % 4, t0:t0 + N], rk="QT",
                              pv=[(acc, V[i][:, kb, :], ("V", i), kb == 0, kb == nkb - 1)]) for kb in range(nkb)]
                pipeline(steps, N, 96 ** -0.5, nsg=3)
                r0 = 512 + h * 64
                epi_bcd(acc, N, [(0, N, self.os_[r0:r0 + 64, t0:t0 + N], [("os", r0 // 128, ci, h % 2)])])
        S.dma("sp", lambda E: E.dma_start(out=QT, in_=self.qs[512:1024, :].rearrange("(b p) t -> p b t", p=128)),
              reads=[k for m in range(4) for k in qkeys(512 + m * 128)], writes=["QT"])
        for j in range(2):
            i = j
            S.op("pool", lambda E: E.memset(KT[i][(1 - j) * 64:(2 - j) * 64, :], 0.0), writes=[("KT", i)])
            load_kt(i, 1280 + j * 64, 64, j * 64)
            load_v(i, 1024 + j * 64, 64)
            for m in range(4):
                for ci in qch:
                    t0, N = CH[ci]
                    nkb = NKB if ci < 4 else 2
                    acc = 6 + st["acc"] % 2
                    st["acc"] += 1
                    steps = [dict(lhsT=KT[i][:, kb * 128:(kb + 1) * 128], lk=("KT", i),
                                  rhs=QT[:, m, t0:t0 + N], rk="QT",
                                  pv=[(acc, V[i][:, kb, :], ("V", i), kb == 0, kb == nkb - 1)]) for kb in range(nkb)]
                    pipeline(steps, N, 0.125, nsg=3)
                    hh = 4 * j + m
                    r0 = 1024 + hh * 64
                    epi_bcd(acc, N, [(0, N, self.os_[r0:r0 + 64, t0:t0 + N], [("os", r0 // 128, ci, hh % 2)])])
        S.dma("sp", lambda E: E.dma_start(out=QT, in_=self.qs[1024:1536, :].rearrange("(b p) t -> p b t", p=128)),
              reads=[k for m in range(4) for k in qkeys(1024 + m * 128)], writes=["QT"])
        flush()
        S.barrier()
        S.dma("pool", lambda E: E.dma_start(out=masks, in_=self.dmask.rearrange("m p c -> p m c")), writes=["masks"])
        KD, KC = KT[0], KT[1]
        DW = NC + NT
        for j in range(2):
            oth = slice((1 - j) * 64, (2 - j) * 64)
            own = slice(j * 64, (j + 1) * 64)
            S.op("pool", lambda E: E.memset(KD[oth, j * DW:(j + 1) * DW], 0.0), writes=[("KT", 0)])
            S.op("pool", lambda E: E.memset(KC[oth, j * 1024:(j + 1) * 1024], 0.0), writes=[("KT", 1)])
            S.dma("sp", lambda E: E.dma_start(out=KD[own, j * DW:j * DW + NC], in_=self.ktc[1408 + j * 64:1472 + j * 64, :]),
                  reads=[("ktc", 1408)], writes=[("KT", 0)])
            S.dma("sp", lambda E: E.dma_start(out=KD[own, j * DW + NC:(j + 1) * DW], in_=self.kdl[own, :]),
                  reads=[("kdl", ci) for ci in range(4)], writes=[("KT", 0)])
            for r in range(4):
                for e in range(2):
                    c0 = j * 1024 + (r * 2 + e) * 128
                    S.dma("sp", lambda E: E.dma_start(out=KC[own, c0:c0 + 128],
                                                      in_=self.recvg["KCD"][r * 256 + 128 + j * 64:r * 256 + 192 + j * 64, e * 1920:e * 1920 + 128]),
                          reads=[("recv", "KCD")], writes=[("KT", 1)])
        for j in range(2):
            S.dma("sp", lambda E: E.dma_start(out=V[j][:, 0:2, 0:64], in_=self.vc[:, 1152 + j * 64:1216 + j * 64].rearrange("(b p) c -> p b c", p=128)),
                  reads=self._vc_keys(), writes=[("V", j)])
            S.dma("sp", lambda E: E.dma_start(out=V[j][:, 2:18, 0:64], in_=self.vdl[:, j * 64:(j + 1) * 64].rearrange("(b p) c -> p b c", p=128)),
                  reads=[("vdl", tt) for tt in range(16)], writes=[("V", j)])
            for r in range(4):
                for e in range(2):
                    S.dma("sp", lambda E: E.dma_start(out=V[j][:, 18 + r * 2 + e, 0:64],
                                                      in_=self.recvg["VCD"][r * NT + e * 1920:r * NT + e * 1920 + 128, 128 + j * 64:192 + j * 64]),
                          reads=[("recv", "VCD")], writes=[("V", j)])
        nq = 16 if last else 18
        for j in range(2):
            for n in range(nq):
                blocks = [(KD[:, j * DW:j * DW + 128], ("KT", 0), 0, None), (KD[:, j * DW + 128:j * DW + 256], ("KT", 0), 1, None)]
                if n < 16:
                    def loc(b, mi):
                        return (KD[:, j * DW + NC + b * 128:j * DW + NC + (b + 1) * 128], ("KT", 0), 2 + b, mi)
                    def cand(r, e, mi):
                        c0 = j * 1024 + (r * 2 + e) * 128
                        return (KC[:, c0:c0 + 128], ("KT", 1), 18 + r * 2 + e, mi)
                    if n > 0:
                        blocks.append(loc(n - 1, 0))
                    else:
                        blocks += [cand(r, 1, 2 + r) for r in range(4)]
                    blocks.append(loc(n, None))
                    if n < 15:
                        blocks.append(loc(n + 1, 1))
                    else:
                        blocks += [cand(r, 0, 6 + r) for r in range(4)]
                acc = 6 + st["acc"] % 2
                st["acc"] += 1
                nb = len(blocks)
                steps = [dict(lhsT=kt, lk=kk, rhs=QT[:, :, n * 128:(n + 1) * 128], rk="QT",
                              mask=(None if mi is None else masks[:, mi, :]),
                              pv=[(acc, V[j][:, vb, :], ("V", j), bi == 0, bi == nb - 1)])
                         for bi, (kt, kk, vb, mi) in enumerate(blocks)]
                pipeline(steps, 512, 0.125, nsg=3)
                ci = min(n // 4, 4)
                dsts = []
                for g in range(4):
                    hh = 4 * j + g
                    r0 = 1536 + hh * 64
                    dsts.append((g * 128, 128, self.os_[r0:r0 + 64, n * 128:(n + 1) * 128], [("os", r0 // 128, ci, hh % 2, n)]))
                epi_bcd(acc, 512, dsts, sink_j=j, delays=(1, 3))
        flush()

    @staticmethod
    def kt_loc(row0):
        if row0 < 512:
            h = row0 // 128
            return f"KA{h // 2}", (h % 2) * 128 + row0 % 128
        if row0 < 1280:
            h = (row0 - 512) // 96
            return f"KB{h // 2}", (h % 2) * 96
        return "KCD", row0 - 1280

    @staticmethod
    def v_loc(col0):
        if col0 < 512:
            return f"VA{col0 // 256}", col0 % 256
        if col0 < 1024:
            return f"VB{(col0 - 512) // 256}", (col0 - 512) % 256
        return "VCD", col0 - 1024

    def gather(self, g):
        rows, cols = self.cg[g]
        self.S.collective(lambda E: E.collective_compute("AllGather", ALU.bypass, replica_groups=[[0, 1, 2, 3], [4, 5, 6, 7]],
                                                         ins=[self.sendg[g].opt()], outs=[self.recvg[g].opt()]),
                          reads=self.sendkeys[g], writes=[("recv", g)])

    def _ktc_keys(self, row0, nrows):
        if row0 < 512 or row0 >= 1280:
            return [("ktc", (row0 // 128) * 128)]
        return [("ktc", row0)]

    def _vc_keys(self):
        return [("vc", "AV", 16), ("vc", "AV", 17), ("vc", "CDV", 16), ("vc", "CDV", 17), ("vc", "B", 16), ("vc", "B", 17)]

    def os_keys(self, ci):
        ks = [("os", h, ci) for h in range(4)]
        ks += [("os", c, ci, e) for c in range(4, 12) for e in range(2)]
        nlist = range(ci * 4, ci * 4 + 4) if ci < 4 else (16, 17)
        ks += [("os", c, ci, e, n) for c in range(12, 16) for e in range(2) for n in nlist]
        return ks

    def p3(self, l, last):
        S = self.S
        Wb = self.A("Wb", [128, 16, D], BF16, 0)
        wo = self.A("wo", [128, 8, D], BF16, 32768)
        oT = [self.A("oT", [128, 16, 512], BF16, 49152 + i * 16384) for i in range(2)]
        gt = [self.A("gt", [128, 4, 512], BF16, 81920 + i * 4096) for i in range(2)]
        tk = [self.A("tk", [128, 512], F32, 90112 + i * 2048) for i in range(4)]
        sT = self.A("sT", [128, 8, 512], BF16, 98304)
        for k in range(4):
            S.dma("pool", lambda E: E.dma_start(out=Wb[:, k * 4:(k + 1) * 4, :], in_=self.wbr[l, k].rearrange("(c p) o -> p c o", p=128)),
                  writes=[("Wb", k)])
        S.dma("pool", lambda E: E.dma_start(out=wo, in_=self.wout[l].rearrange("(c p) o -> p c o", p=128)), writes=["wo"])
        qch = list(range(4)) if last else list(range(5))
        gc = 0
        for n_i, ci in enumerate(qch):
            t0, N = CH[ci]
            jm = 1 if ci == 4 else 0
            o = oT[n_i % 2]
            S.dma("sp", lambda E: E.dma_start(out=o[:, :, :N], in_=self.os_[:, t0:t0 + N].rearrange("(c p) t -> p c t", p=128)),
                  reads=self.os_keys(ci), writes=[("oT", n_i % 2)])
            for j in range(8):
                g = gt[gc % 2]
                gk = ("gt", gc % 2)
                gc += 1
                S.dma("sp", lambda E: E.dma_start(out=g[:, :, :N],
                                                  in_=self.gs.rearrange("(k j p) t -> p k j t", k=4, j=8)[:, :, j, t0:t0 + N]),
                      reads=[("gs", k * 8 + j, ci) for k in range(4)], writes=[gk])
                for k in range(4):
                    bank = k
                    for c in range(4):
                        S.op("pe", lambda E: E.matmul(self.ps[bank][:, :N], lhsT=Wb[:, k * 4 + c, j * 128:(j + 1) * 128], rhs=o[:, k * 4 + c, :N],
                                                      start=(c == 0), stop=(c == 3)), reads=[("Wb", k), ("oT", n_i % 2)], writes=[("ps", bank)])
                    S.op("dve", lambda E: E.tensor_tensor(out=tk[k][:, :N], in0=self.ps[bank][:, :N], in1=g[:, k, :N], op=ALU.mult),
                         reads=[("ps", bank), gk], writes=[("tk", k)])
                S.op("pool", lambda E: E.tensor_tensor(out=tk[0][:, :N], in0=tk[0][:, :N], in1=tk[1][:, :N], op=ALU.add),
                     reads=[("tk", 0), ("tk", 1)], writes=[("tk", 0)])
                S.op("pool", lambda E: E.tensor_tensor(out=tk[2][:, :N], in0=tk[2][:, :N], in1=tk[3][:, :N], op=ALU.add),
                     reads=[("tk", 2), ("tk", 3)], writes=[("tk", 2)])
                S.op("dve", lambda E: E.tensor_tensor(out=sT[:, j, :N], in0=tk[0][:, :N], in1=tk[2][:, :N], op=ALU.add),
                     reads=[("tk", 0), ("tk", 2)], writes=[("sT", j)])
            for i in range(8):
                bank = 4 + i % 4
                for j in range(8):
                    S.op("pe", lambda E: E.matmul(self.ps[bank][:, :N], lhsT=wo[:, j, i * 128:(i + 1) * 128], rhs=sT[:, j, :N],
                                                  start=(j == 0), stop=(j == 7)), reads=["wo", ("sT", j)], writes=[("ps", bank)])
                S.op("dve", lambda E: E.scalar_tensor_tensor(out=self.xT[:, i, t0:t0 + N], in0=self.ps[bank][:, :N], scalar=self.mod(l, 2, i, jm),
                                                             in1=self.xT[:, i, t0:t0 + N], op0=ALU.mult, op1=ALU.add),
                     reads=[("ps", bank), ("xT", ci), "modT"], writes=[("xT", ci)])
        self.dump(f"xm{l}", self.xT, [("xT", c) for c in range(5)])

    def p4(self, l, last):
        S = self.S
        H2 = self.A("H2T", [128, 8, 2308], BF16, 0)
        WU = [self.A("WU", [128, 8, 1024], BF16, 36992 + i * 16384) for i in range(2)]
        WD = [self.A("WD", [128, 4, D], BF16, 69760 + i * 8192) for i in range(2)]
        T = [self.A("p4t", [128, 512], F32, 86144 + i * 2048) for i in range(6)]
        aT = [self.A("aT", [128, 4, 512], BF16, 98432 + i * 4096) for i in range(2)]
        cand = self.A("cand", [128, 4, 2, 8], BF16, 114816)
        flg = self.A("flg", [128, 2, 4, 8], F32, 114944)
        ctmp = self.A("ctmp", [128, 4, 8], F32, 115200)
        ctmp2 = self.A("ctmp2", [128, 8], F32, 115328)
        qch = list(range(4)) if last else list(range(5))
        dcol = lambda ci: 1 + CH[ci][0] if ci < 4 else 2051
        self.modulate(l, 3, 4, H2, dcol, "H2", qch, 104576, 108672, 110720)
        S.barrier()
        stg = self.A("stg", [128, 2, 8], BF16, 115392)
        S.op("dve", lambda E: E.tensor_copy(out=stg[:, 0, :], in_=H2[:, :, 1]), reads=[("H2", 0)], writes=["stg"])
        S.op("dve", lambda E: E.tensor_copy(out=stg[:, 1, :], in_=H2[:, :, 2048]), reads=[("H2", 3)], writes=["stg"])
        S.dma("sp", lambda E: E.dma_start(out=self.send2, in_=stg.rearrange("p e k -> p (e k)")), reads=["stg"], writes=["send2"])
        S.collective(lambda E: E.collective_compute("AllGather", ALU.bypass, replica_groups=[[0, 1, 2, 3], [4, 5, 6, 7]],
                                                    ins=[self.send2.opt()], outs=[self.recv2.opt()]),
                     reads=["send2"], writes=["recv2"])
        S.dma("sp", lambda E: E.dma_start(out=cand, in_=self.recv2.rearrange("(r p) (e k) -> p r e k", p=128, e=2)), reads=["recv2"], writes=["cand"])
        S.dma("sp", lambda E: E.dma_start(out=flg, in_=self.hflag.rearrange("a p r k -> p a r k")), writes=["flg"])
        for a, e, col in ((0, 1, 0), (1, 0, 2049)):
            S.op("dve", lambda E: E.tensor_tensor(out=ctmp, in0=cand[:, :, e, :], in1=flg[:, a], op=ALU.mult),
                 reads=["cand", "flg"], writes=["ctmp"])
            S.op("dve", lambda E: E.tensor_reduce(out=ctmp2, in_=ctmp.rearrange("p r k -> p k r"), axis=mybir.AxisListType.X, op=ALU.add),
                 reads=["ctmp"], writes=["ctmp2"])
            S.op("dve", lambda E: E.tensor_copy(out=H2[:, :, col], in_=ctmp2), reads=["ctmp2"], writes=[("H2h", col)])
        S.op("pool", lambda E: E.memset(H2[:, :, 2050:2051], 0.0), writes=[("H2h", 2050)])
        S.op("pool", lambda E: E.memset(H2[:, :, 2307:2308], 0.0), writes=[("H2h", 2307)])
        tch = []
        for s0 in (510, 1020, 1530, 0, 2040):
            n = min(510, NT - s0)
            xk = sorted(set([s0 // 512, (s0 + n - 1) // 512]))
            hk = [("H2", c) for c in sorted(set([max(s0 - 1, 0) // 512, min(s0 + n, NT - 1) // 512]))]
            if s0 == 0:
                hk.append(("H2h", 0))
            if s0 + n == NT:
                hk.append(("H2h", 2049))
            tch.append((s0, n, s0, 0, xk, hk))
        if not last:
            tch.append((2050, 256, 2048, 1, [4], [("H2", 4), ("H2h", 2050), ("H2h", 2307)]))
        groups = [(0, 4), (4, 4), (8, 4), (12, 4), (16, 4), (20, 2)]
        cb = PL0 + l * PLN
        ac = 0
        def load_ffn_w(gi):
            c0, ncg = groups[gi]
            wu, wd = WU[gi % 2], WD[gi % 2]
            S.dma("pool", lambda E: E.dma_start(out=wu[:, :, 0:ncg * 128],
                                                in_=self.wup[l, :, c0 * 128:(c0 + ncg) * 128].rearrange("(kc p) c -> p kc c", p=128)),
                  writes=[("WU", gi % 2)])
            S.dma("pool", lambda E: E.dma_start(out=wu[:, :, 512:512 + ncg * 128],
                                                in_=self.wup[l, :, DFF + c0 * 128:DFF + (c0 + ncg) * 128].rearrange("(kc p) c -> p kc c", p=128)),
                  writes=[("WU", gi % 2)])
            S.dma("pool", lambda E: E.dma_start(out=wd[:, 0:ncg, :], in_=self.wdn[l, c0 * 128:(c0 + ncg) * 128, :].rearrange("(c p) o -> p c o", p=128)),
                  writes=[("WD", gi % 2)])
        load_ffn_w(0)
        for gi, (c0, ncg) in enumerate(groups):
            wu, wd = WU[gi % 2], WD[gi % 2]
            if gi + 1 < len(groups):
                load_ffn_w(gi + 1)
            for (a0, n, x0, jm, xkeys, HKALL) in tch:
                at = aT[ac % 2]
                ak = ("aT", ac % 2)
                ac += 1
                for cc in range(ncg):
                    ch = c0 + cc
                    pv, pg = self.ps[(cc % 2) * 2], self.ps[(cc % 2) * 2 + 1]
                    for (ps_, bank, wc0) in ((pv, (cc % 2) * 2, cc * 128), (pg, (cc % 2) * 2 + 1, 512 + cc * 128)):
                        for kc in range(8):
                            S.op("pe", lambda E: E.matmul(ps_[:, :n + 2], lhsT=wu[:, kc, wc0:wc0 + 128], rhs=H2[:, kc, a0:a0 + n + 2],
                                                          start=(kc == 0), stop=(kc == 7)), reads=[("WU", gi % 2)] + HKALL, writes=[("ps", bank)])
                    tb = (cc % 2) * 3
                    for (ps_, bank, cidx, t, tkey) in ((pv, (cc % 2) * 2, ch, T[tb], ("p4t", tb)), (pg, (cc % 2) * 2 + 1, 22 + ch, T[tb + 1], ("p4t", tb + 1))):
                        w0 = self.ppt[:, cb + PLO["convw"] + cidx * 3 + 0:cb + PLO["convw"] + cidx * 3 + 1]
                        w1 = self.ppt[:, cb + PLO["convw"] + cidx * 3 + 1:cb + PLO["convw"] + cidx * 3 + 2]
                        w2 = self.ppt[:, cb + PLO["convw"] + cidx * 3 + 2:cb + PLO["convw"] + cidx * 3 + 3]
                        bb = self.ppt[:, cb + PLO["convb"] + cidx:cb + PLO["convb"] + cidx + 1]
                        S.op("dve", lambda E: E.tensor_scalar(out=t[:, :n], in0=ps_[:, 1:n + 1], scalar1=w1, scalar2=bb, op0=ALU.mult, op1=ALU.add),
                             reads=[("ps", bank), "pp"], writes=[tkey])
                        S.op("dve", lambda E: E.scalar_tensor_tensor(out=t[:, :n], in0=ps_[:, 0:n], scalar=w0, in1=t[:, :n], op0=ALU.mult, op1=ALU.add),
                             reads=[("ps", bank), "pp", tkey], writes=[tkey])
                        S.op("dve", lambda E: E.scalar_tensor_tensor(out=t[:, :n], in0=ps_[:, 2:n + 2], scalar=w2, in1=t[:, :n], op0=ALU.mult, op1=ALU.add),
                             reads=[("ps", bank), "pp", tkey], writes=[tkey])
                    S.op("act", lambda E: E.activation(out=T[tb + 2][:, :n], in_=T[tb + 1][:, :n], func=AF.Silu), reads=[("p4t", tb + 1)], writes=[("p4t", tb + 2)])
                    S.op("pool", lambda E: E.tensor_tensor(out=at[:, cc, :n], in0=T[tb][:, :n], in1=T[tb + 2][:, :n], op=ALU.mult),
                         reads=[("p4t", tb), ("p4t", tb + 2)], writes=[ak])
                for i in range(8):
                    bank = 4 + i % 4
                    for cc in range(ncg):
                        S.op("pe", lambda E: E.matmul(self.ps[bank][:, :n], lhsT=wd[:, cc, i * 128:(i + 1) * 128], rhs=at[:, cc, :n],
                                                      start=(cc == 0), stop=(cc == ncg - 1)), reads=[("WD", gi % 2), ak], writes=[("ps", bank)])
                    S.op("dve", lambda E: E.scalar_tensor_tensor(out=self.xT[:, i, x0:x0 + n], in0=self.ps[bank][:, :n], scalar=self.mod(l, 5, i, jm),
                                                                 in1=self.xT[:, i, x0:x0 + n], op0=ALU.mult, op1=ALU.add),
                         reads=[("ps", bank), "modT"] + [("xT", c) for c in xkeys], writes=[("xT", c) for c in xkeys])
        self.dump(f"x{l + 1}", self.xT, [("xT", c) for c in range(5)])


PP_EPS = 0
PP_FN = 1
PL0 = 9
PLO = {}
_o = 0
for _n, _w in (("convw", 132), ("convb", 44), ("qn", 2), ("kvn", 2), ("cqg", 1), ("cqgs", 1), ("ckg", 1), ("ckgs", 1),
               ("subl", 1), ("lam", 256), ("sink", 8)):
    PLO[_n] = _o
    _o += _w
PLN = _o
NPP = PL0 + NL * PLN


def _consts():
    c = np.zeros((4, 128, 128), np.float32)
    c[0] = np.eye(128)
    c[1] = 1.0
    c[2, :64, :64] = 1.0
    c[2, 64:, 64:] = 1.0
    for m in range(64):
        c[3, 64 + m, m] = 1.0
    return c


def _fm(v):
    v = np.asarray(v, np.float32)
    return np.ascontiguousarray(v.reshape(-1, 128).T)


_CACHE = {}


def _get_nc(nlayers=NL, dbg=None):
    key = (nlayers, tuple(d[0] for d in (dbg or [])))
    if key not in _CACHE:
        _CACHE[key] = Builder(nlayers, dbg).nc
    return _CACHE[key]


def _host_inputs(inp):
    x = np.asarray(inp["x"], np.float32)
    maps = []
    w_in = np.asarray(inp["w_in"], np.float32)
    wf = np.zeros((NL, D, NWF), np.float32)
    valid = WF_COLS >= 0
    wf[:, :, valid] = w_in[:, :, WF_COLS[valid]]
    shared = {
        "w_mod": np.ascontiguousarray(inp["w_mod"], np.float32),
        "wf": wf,
        "w_branch": np.ascontiguousarray(inp["w_branch"], np.float32),
        "w_out": np.ascontiguousarray(inp["w_out"], np.float32),
        "w_up": np.ascontiguousarray(inp["ffn_w_up"], np.float32),
        "w_down": np.ascontiguousarray(inp["ffn_w_down"], np.float32),
        "consts": _consts(),
    }
    bm = np.asarray(inp["b_mod"], np.float32)
    shared["bmodT"] = np.ascontiguousarray(
        np.repeat(bm.reshape(NL, 48, 128).transpose(0, 2, 1)[:, :, :, None], 2, axis=3))
    pp = np.zeros((128, NPP), np.float32)
    pp[:, PP_EPS] = EPS
    pp[:, PP_FN:PP_FN + 8] = _fm(inp["final_norm"])
    shared["pp"] = pp
    wuq = np.ascontiguousarray(inp["mla_w_uq"], np.float32)
    idx = np.arange(768)
    hh, dd = idx // 96, idx % 96
    sidx = np.where(dd >= 64, hh * 96 + 64 + ((dd - 64) ^ 1), idx)
    shared["wuq"] = wuq
    shared["wuqs"] = np.ascontiguousarray(wuq[:, :, sidx])
    wukv = np.asarray(inp["mla_w_ukv"], np.float32).reshape(NL, 256, 8, 128)
    shared["wkn"] = np.ascontiguousarray(wukv[:, :, :, :64].reshape(NL, 256, 512))
    shared["wvb"] = np.ascontiguousarray(wukv[:, :, :, 64:].reshape(NL, 256, 512))
    p64 = np.arange(128) % 64
    for l in range(NL):
        b0 = PL0 + l * PLN
        cw = np.asarray(inp["ffn_conv_w"][l], np.float32)
        for k in range(3):
            pp[:, b0 + PLO["convw"] + k:b0 + PLO["convw"] + 132:3] = _fm(cw[k])
        pp[:, b0 + PLO["convb"]:b0 + PLO["convb"] + 44] = _fm(inp["ffn_conv_b"][l])
        pp[:, b0 + PLO["qn"]:b0 + PLO["qn"] + 2] = _fm(inp["mla_q_norm"][l])
        pp[:, b0 + PLO["kvn"]:b0 + PLO["kvn"] + 2] = _fm(inp["mla_kv_norm"][l])
        qg = np.asarray(inp["gqa_q_norm"][l], np.float32)
        kg = np.asarray(inp["gqa_k_norm"][l], np.float32)
        pp[:, b0 + PLO["cqg"]] = qg[p64]
        pp[:, b0 + PLO["cqgs"]] = qg[p64 ^ 1]
        pp[:, b0 + PLO["ckg"]] = kg[p64]
        pp[:, b0 + PLO["ckgs"]] = kg[p64 ^ 1]
        pp[:, b0 + PLO["subl"]] = np.asarray(inp["diff_subln"][l], np.float32)
        pp[:, b0 + PLO["lam"]:b0 + PLO["lam"] + 256] = np.asarray(inp["diff_lambda"][l], np.float32).reshape(1, 256)
        pp[:, b0 + PLO["sink"]:b0 + PLO["sink"] + 8] = np.asarray(inp["swa_sink"][l], np.float32).reshape(1, 8)
    kk = np.arange(128)[:, None]
    qq = np.arange(128)[None, :]
    lo = np.where(kk >= qq, 0.0, NEGM).astype(np.float32)
    hi = np.where(kk <= qq, 0.0, NEGM).astype(np.float32)
    lo4, hi4 = np.tile(lo, (1, 4)), np.tile(hi, (1, 4))
    neg4 = np.full((128, 512), NEGM, np.float32)
    for core in range(8):
        b, r = core // 4, core % 4
        m = dict(shared)
        t = (r * NT + np.arange(NT)).astype(np.int32)
        row = (t // 64).astype(np.float32)
        col = (t % 64).astype(np.float32)
        def tables(axis_dim):
            inv = (np.float32(10000.0) ** (-(np.arange(0, axis_dim, 2, dtype=np.float32)) / np.float32(axis_dim))).astype(np.float32)
            ang = np.concatenate([row[:, None] * inv, col[:, None] * inv], axis=-1).astype(np.float32)
            return np.cos(ang).astype(np.float32), np.sin(ang).astype(np.float32)
        c64, s64 = tables(32)
        c32, s32 = tables(16)
        r64 = np.zeros((2, 128, TOK), np.float32)
        r64[0, :, NT:] = 1.0
        d = np.arange(64)
        sign = np.where(d % 2 == 0, -1.0, 1.0).astype(np.float32)
        for rep in range(2):
            r64[0, rep * 64:(rep + 1) * 64, :NT] = c64[:, d // 2].T
            r64[1, rep * 64:(rep + 1) * 64, :NT] = (s64[:, d // 2] * sign[None, :]).T
        r32 = np.zeros((2, 96, TOK), np.float32)
        r32[0] = 1.0
        d2 = np.arange(32)
        sign2 = np.where(d2 % 2 == 0, -1.0, 1.0).astype(np.float32)
        r32[0, 64:, :NT] = c32[:, d2 // 2].T
        r32[1, 64:, :NT] = (s32[:, d2 // 2] * sign2[None, :]).T
        m["rope64"] = r64
        m["rope32"] = r32
        dm = np.zeros((10, 128, 512), np.float32)
        dm[0], dm[1] = lo4, hi4
        hf = np.zeros((2, 128, 4, 8), np.float32)
        for rr in range(4):
            dm[2 + rr] = lo4 if rr == r - 1 else neg4
            dm[6 + rr] = hi4 if rr == r + 1 else neg4
            hf[0, :, rr, :] = 1.0 if rr == r - 1 else 0.0
            hf[1, :, rr, :] = 1.0 if rr == r + 1 else 0.0
        m["dmask"] = dm
        m["hflag"] = hf
        m["x"] = np.ascontiguousarray(x[b, r * NT:(r + 1) * NT])
        m["ctx"] = np.ascontiguousarray(inp["ctx"][b], np.float32)
        cT = np.stack([_fm(inp["c"][b]), _fm(inp["c_ctx"])], axis=-1)
        m["cT"] = np.ascontiguousarray(cT, np.float32)
        maps.append(m)
    return maps


def kernel(**inputs):
    nc = _get_nc()
    maps = _host_inputs(inputs)
    names = set(t for t in _input_names(nc))
    maps = [{k: v for k, v in m.items() if k in names} for m in maps]
    res = run_bass_kernel_spmd(nc, maps, core_ids=list(range(8)))
    out = np.zeros((2, 8192, D), np.float32)
    for core in range(8):
        b, r = core // 4, core % 4
        out[b, r * NT:(r + 1) * NT] = res.results[core]["y"]
    return out


def _input_names(nc):
    return ["x", "ctx", "cT", "w_mod", "bmodT", "wf", "wuq", "wuqs", "wkn", "wvb", "w_branch", "w_out",
            "w_up", "w_down", "pp", "rope64", "rope32", "dmask", "hflag", "consts"]
```
